# Optimizing a Trainium2 kernel written in Bass

```python
import math
import jax, jax.numpy as jnp
from jax import lax
import numpy as np

D_MODEL = 1024
BATCH = 8
SEQ = 4096
DEPTH = 1

CHUNK = 64
MIX_A_WIDTH = D_MODEL // 2
MIX_B_WIDTH = D_MODEL - MIX_A_WIDTH
A_GROUPS = 8
A_GROUP_DIM = MIX_A_WIDTH // A_GROUPS
A_BLOCK = 128
B_HEADS = 8
B_HEAD_DIM = MIX_B_WIDTH // B_HEADS
B_LEFT_CHUNKS = 8
B_BAND = (B_LEFT_CHUNKS + 1) * CHUNK
REL_CLIP = 128
PEER_HEADS = 8
PEER_NKEYS = 128
PEER_EXPERTS = PEER_NKEYS * PEER_NKEYS
PEER_QDIM = 256
PEER_HALF = PEER_QDIM // 2
PEER_TOPK = 16
PEER_TOKEN_BLOCK = 128
IN_COLS = 2 * MIX_A_WIDTH + 3 * MIX_B_WIDTH
EPS = 1e-6
NEG_INF = -1e30

kernel_name = "hybrid_gmlp_chunkattn_peer_block"


def rms_norm(x, g):
    xf = x.astype(jnp.float32)
    y = xf * lax.rsqrt(jnp.mean(xf * xf, axis=-1, keepdims=True) + EPS)
    return (y * g.astype(jnp.float32)).astype(x.dtype)


def gmlp_spatial_gating(u, v, norm_g, w_s, b_s):
    bsz, s, _ = u.shape
    u = jax.nn.gelu(u)
    v = rms_norm(jax.nn.gelu(v), norm_g)
    nblk = s // A_BLOCK
    v = v.reshape(bsz, nblk, A_BLOCK, A_GROUPS, A_GROUP_DIM)
    pos = jnp.arange(A_BLOCK)
    mask = (pos[None, :] // CHUNK) <= (pos[:, None] // CHUNK)
    w = jnp.where(mask[None], w_s, jnp.zeros_like(w_s)).astype(v.dtype)
    sp = jnp.einsum('gij,bnjgc->bnigc', w, v) + b_s.T.astype(v.dtype)[None, None, :, :, None]
    return u * sp.reshape(bsz, s, MIX_A_WIDTH)


def banded_chunk_attention(q, k, v, q_g, k_g, rel_bias):
    bsz, s, _ = q.shape
    nc = s // CHUNK
    pad = B_LEFT_CHUNKS * CHUNK
    q = rms_norm(q.reshape(bsz, nc, CHUNK, B_HEADS, B_HEAD_DIM), q_g)
    k = rms_norm(k.reshape(bsz, s, B_HEADS, B_HEAD_DIM), k_g)
    v = v.reshape(bsz, s, B_HEADS, B_HEAD_DIM)
    padw = ((0, 0), (pad, 0), (0, 0), (0, 0))
    kp = jnp.pad(k, padw).reshape(bsz, nc + B_LEFT_CHUNKS, CHUNK, B_HEADS, B_HEAD_DIM)
    vp = jnp.pad(v, padw).reshape(bsz, nc + B_LEFT_CHUNKS, CHUNK, B_HEADS, B_HEAD_DIM)
    band_idx = jnp.arange(nc)[:, None] + jnp.arange(B_LEFT_CHUNKS + 1)[None, :]
    kb = kp[:, band_idx].reshape(bsz, nc, B_BAND, B_HEADS, B_HEAD_DIM)
    vb = vp[:, band_idx].reshape(bsz, nc, B_BAND, B_HEADS, B_HEAD_DIM)
    scores = jnp.einsum('bcihd,bcjhd->bhcij', q, kb).astype(jnp.float32) * (B_HEAD_DIM ** -0.5)
    qi = jnp.arange(CHUNK)[:, None] + pad
    kj = jnp.arange(B_BAND)[None, :]
    rel_idx = jnp.clip(qi - kj, -REL_CLIP, REL_CLIP) + REL_CLIP
    bias = rel_bias.astype(jnp.float32)[:, rel_idx]
    valid = kj >= (pad - jnp.arange(nc)[:, None] * CHUNK)
    scores = scores + bias[None, :, None]
    scores = jnp.where(valid[None, None, :, None, :], scores, NEG_INF)
    p = jax.nn.softmax(scores, axis=-1).astype(v.dtype)
    out = jnp.einsum('bhcij,bcjhd->bcihd', p, vb)
    return out.reshape(bsz, s, MIX_B_WIDTH)


def peer_ffn(x, w_query, sub_keys, expert_u, expert_v):
    bsz, s, d = x.shape
    tokens = x.reshape(-1, PEER_TOKEN_BLOCK, d)

    def block(xb):
        q = (xb @ w_query).reshape(PEER_TOKEN_BLOCK, PEER_HEADS, 2, PEER_HALF)
        sc = jnp.einsum('thpd,hpkd->thpk', q, sub_keys).astype(jnp.float32)
        s_top, i_top = lax.top_k(sc, PEER_TOPK)
        cand = s_top[:, :, 0, :, None] + s_top[:, :, 1, None, :]
        cand_idx = i_top[:, :, 0, :, None] * PEER_NKEYS + i_top[:, :, 1, None, :]
        cand = cand.reshape(PEER_TOKEN_BLOCK, PEER_HEADS, PEER_TOPK * PEER_TOPK)
        cand_idx = cand_idx.reshape(PEER_TOKEN_BLOCK, PEER_HEADS, PEER_TOPK * PEER_TOPK)
        best, pos = lax.top_k(cand, PEER_TOPK)
        eidx = jnp.take_along_axis(cand_idx, pos, axis=-1)
        g = jax.nn.softmax(best, axis=-1)
        u = jnp.take(expert_u, eidx, axis=0)
        h = jax.nn.gelu(jnp.einsum('thkd,td->thk', u, xb))
        vv = jnp.take(expert_v, eidx, axis=0)
        return jnp.einsum('thk,thkd->td', (g * h.astype(jnp.float32)).astype(xb.dtype), vv)

    return lax.map(block, tokens).reshape(bsz, s, d)


def setup_inputs(seed: int = 0) -> dict:
    key = jax.random.key(seed)
    ks = jax.random.split(key, 16)
    f32 = jnp.float32
    nrm = lambda k, shape, sc: jax.random.normal(k, shape, f32) * sc
    return {
        "x": nrm(ks[0], (BATCH, SEQ, D_MODEL), 1.0),
        "ln_mix_g": 1.0 + nrm(ks[1], (DEPTH, D_MODEL), 0.02),
        "w_in": nrm(ks[2], (DEPTH, D_MODEL, IN_COLS), D_MODEL ** -0.5),
        "gmlp_norm_g": 1.0 + nrm(ks[3], (DEPTH, MIX_A_WIDTH), 0.02),
        "gmlp_w_s": nrm(ks[4], (DEPTH, A_GROUPS, A_BLOCK, A_BLOCK), A_BLOCK ** -0.5),
        "gmlp_b_s": 1.0 + nrm(ks[5], (DEPTH, A_GROUPS, A_BLOCK), 0.1),
        "q_norm_g": 1.0 + nrm(ks[6], (DEPTH, B_HEAD_DIM), 0.02),
        "k_norm_g": 1.0 + nrm(ks[7], (DEPTH, B_HEAD_DIM), 0.02),
        "rel_bias": nrm(ks[8], (DEPTH, B_HEADS, 2 * REL_CLIP + 1), 0.1),
        "w_out": nrm(ks[9], (DEPTH, D_MODEL, D_MODEL), D_MODEL ** -0.5),
        "ln_ffn_g": 1.0 + nrm(ks[10], (DEPTH, D_MODEL), 0.02),
        "peer_w_query": nrm(ks[11], (DEPTH, D_MODEL, PEER_HEADS * PEER_QDIM), D_MODEL ** -0.5),
        "peer_sub_keys": nrm(ks[12], (DEPTH, PEER_HEADS, 2, PEER_NKEYS, PEER_HALF), PEER_HALF ** -0.5),
        "peer_u": nrm(ks[13], (DEPTH, PEER_EXPERTS, D_MODEL), D_MODEL ** -0.5),
        "peer_v": nrm(ks[14], (DEPTH, PEER_EXPERTS, D_MODEL), 0.1),
    }


def reference(x, ln_mix_g, w_in, gmlp_norm_g, gmlp_w_s, gmlp_b_s, q_norm_g, k_norm_g, rel_bias,
              w_out, ln_ffn_g, peer_w_query, peer_sub_keys, peer_u, peer_v):
    a0, a1 = MIX_A_WIDTH, 2 * MIX_A_WIDTH
    q0, k0, v0 = a1, a1 + MIX_B_WIDTH, a1 + 2 * MIX_B_WIDTH
    for l in range(DEPTH):
        xn = rms_norm(x, ln_mix_g[l])
        proj = xn @ w_in[l]
        a_out = gmlp_spatial_gating(proj[..., :a0], proj[..., a0:a1],
                                    gmlp_norm_g[l], gmlp_w_s[l], gmlp_b_s[l])
        b_out = banded_chunk_attention(proj[..., q0:k0], proj[..., k0:v0], proj[..., v0:],
                                       q_norm_g[l], k_norm_g[l], rel_bias[l])
        x = x + jnp.concatenate([a_out, b_out], axis=-1) @ w_out[l]
        x = x + peer_ffn(rms_norm(x, ln_ffn_g[l]), peer_w_query[l], peer_sub_keys[l],
                         peer_u[l], peer_v[l])
    return x
```

```python
import contextlib
import numpy as np
import concourse.bass as bass
import concourse.mybir as mybir
from concourse.bass_utils import run_bass_kernel_spmd

F32 = mybir.dt.float32
BF16 = mybir.dt.bfloat16
I32 = mybir.dt.int32
U32 = mybir.dt.uint32
ALU = mybir.AluOpType
AF = mybir.ActivationFunctionType
AX = mybir.AxisListType

EPOCH = 24000
EPS = 1e-6
NSLOT = 5
NG = 14
ND = 4
RATIO = 0.85
NW = 3
D = 1024


class Sched:
    ENGS = ("pe", "act", "dve", "pool", "sp")

    def __init__(self, nc):
        self.nc = nc
        self.stream = {e: [] for e in self.ENGS}
        self.count = {e: 0 for e in self.ENGS}
        self.waited = {e: {} for e in self.ENGS}
        self.writers = {}
        self.readers = {}
        self.dma_count = {}
        self.semkeys = []
        self.semset = set()
        self.group = set()
        self.sealed = set()

    def _key(self, k):
        if k not in self.semset:
            self.semset.add(k)
            self.semkeys.append(k)

    def _need(self, eng, tok, waits):
        key, val = tok
        if eng == "pe" and key[0] == "pe":
            return
        if key in self.group:
            val = self.dma_count[key[1]]
            self.sealed.add(key)
        if self.waited[eng].get(key, 0) < val:
            self.waited[eng][key] = val
            waits[key] = max(waits.get(key, 0), val)

    def _deps(self, eng, reads, writes, me=None):
        waits = {}
        for r in reads:
            for tok in self.writers.get(r, {}).values():
                self._need(eng, tok, waits)
        for w in writes:
            part = w.endswith("+")
            r = w[:-1] if part else w
            for weng, tok in self.writers.get(r, {}).items():
                if part and (weng == eng or weng == me):
                    continue
                self._need(eng, tok, waits)
            for tok in self.readers.get(r, {}).values():
                self._need(eng, tok, waits)
        return waits

    def _commit(self, eng, tok, reads, writes):
        for r in reads:
            self.readers.setdefault(r, {})[(eng, tok[0])] = tok
        for w in writes:
            part = w.endswith("+")
            r = w[:-1] if part else w
            if part:
                self.writers.setdefault(r, {})[eng] = tok
            else:
                self.writers[r] = {eng: tok}
            self.readers[r] = {}

    def op(self, eng, fn, reads=(), writes=()):
        waits = self._deps(eng, reads, writes)
        n = self.count[eng]
        self.count[eng] = n + 1
        key = (eng, n // EPOCH)
        self._key(key)
        tok = (key, n % EPOCH + 1)
        self.stream[eng].append((waits, fn, key, 1))
        self._commit(eng, tok, reads, writes)
        return tok

    def dma(self, eng, fn, sem, reads=(), writes=(), group=False):
        key = ("dma", sem)
        self._key(key)
        if group:
            self.group.add(key)
            assert key not in self.sealed, "group sem %s already waited on" % sem
        waits = self._deps(eng, reads, writes, me="dma:" + sem)
        v = self.dma_count.get(sem, 0) + 16
        self.dma_count[sem] = v
        tok = (key, v)
        self.stream[eng].append((waits, fn, key, 16))
        self._commit("dma:" + sem, tok, reads, writes)
        return tok

    def final_wait(self, eng, toks):
        waits = {}
        for tok in toks:
            self._need(eng, tok, waits)
        self.stream[eng].append((waits, None, None, 0))

    def emit(self):
        nc = self.nc
        with contextlib.ExitStack() as st:
            sems = {}
            for k in self.semkeys:
                sems[k] = st.enter_context(nc.semaphore("s_%s_%s" % (k[0], k[1])))
            block = st.enter_context(nc.Block())

            def run(engname):
                def body(e):
                    for waits, fn, key, inc in self.stream[engname]:
                        for wk, wv in waits.items():
                            e.wait_ge(sems[wk], wv)
                        if fn is not None:
                            fn(e).then_inc(sems[key], inc)
                return body

            block.tensor(run("pe"))
            block.scalar(run("act"))
            block.vector(run("dve"))
            block.gpsimd(run("pool"))
            block.sync(run("sp"))


def I(method, *a, **k):
    return lambda e: getattr(e, method)(*a, **k)


def build_nc(NT=32, interleave=True):
    nc = bass.Bass("TRN2", target_bir_lowering=False)
    NTOK = NT * 128

    def din(name, shape):
        return nc.dram_tensor(name, shape, F32, kind="ExternalInput")

    x_h = din("x", [NTOK, D])
    g1_h = din("ln_mix_g", [1, D])
    win_h = din("w_in", [D, 2560])
    ng_h = din("gmlp_norm_g", [1, 512])
    ws_h = din("gmlp_w_s", [1024, 128])
    bs_h = din("gmlp_b_s", [8, 128])
    qg_h = din("q_norm_g", [1, 64])
    kg_h = din("k_norm_g", [1, 64])
    rb_h = din("rel_bias", [8, 257])
    wout_h = din("w_out", [D, D])
    g2_h = din("ln_ffn_g", [1, D])
    wq_h = din("peer_w_query", [D, 2048])
    sk_h = din("peer_sub_keys", [2048, 128])
    pu_h = din("peer_u", [16384, D])
    pv_h = din("peer_v", [16384, D])
    y_h = nc.dram_tensor("y", [NTOK, D], F32, kind="ExternalOutput")
    wch_h = nc.dram_tensor("wch", [18, 128, 2048], BF16, kind="Internal")
    rbD_h = nc.dram_tensor("rbD", [128, 3072], F32, kind="Internal")
    uv_h = nc.dram_tensor("uv", [16384, 2 * D], BF16, kind="Internal")

    x_ap, y_ap = x_h.ap(), y_h.ap()
    wch_ap = wch_h.ap()
    pu_ap, pv_ap = pu_h.ap(), pv_h.ap()
    uv_ap = uv_h.ap()
    wout_ap = wout_h.ap()

    def dap(h, off, ap):
        return bass.AP(tensor=h, offset=off, ap=ap)

    with contextlib.ExitStack() as st:
        def sb(name, shape, dt):
            return st.enter_context(nc.sbuf_tensor(name, shape, dt))

        wout = sb("wout", [128, 8, D], BF16)
        skT = sb("skT", [128, 16, 128], BF16)
        wmT = sb("wmT", [128, 8, 128], BF16)
        EB = sb("EB", [128, 8, 640], BF16)
        g1b = sb("g1b", [128, D], F32)
        g2b = sb("g2b", [128, D], F32)
        ngf = sb("ngf", [128, 512], F32)
        bsf = sb("bsf", [128, 512], F32)
        bsT = sb("bsT", [128, 8], F32)
        c8 = sb("c8", [128, 8], F32)
        qgb = sb("qgb", [128, 512], F32)
        kgb = sb("kgb", [128, 512], F32)
        identf = sb("identf", [128, 128], F32)
        identb = sb("identb", [128, 128], BF16)
        iotaf = sb("iotaf", [128, 128], F32)
        pidx = sb("pidx", [128, 1], F32)
        WBUF = [sb("wbuf%d" % i, [128, 8, 256], BF16) for i in range(NW)]
        ACC = [sb("acc%d" % i, [128, D], F32) for i in range(2)]
        gbuf = sb("gbuf", [128, NG, 2 * D], BF16)
        DIAG = [sb("diag%d" % i, [128, 128], BF16) for i in range(ND)]
        junkD = sb("junkD", [128, D], BF16)
        ss1 = sb("ss1", [128, 1], F32)
        rs1 = sb("rs1", [128, 1], F32)
        xnb = sb("xnb", [128, D], BF16)
        T8 = sb("T8", [128, D], BF16)
        ssv = sb("ssv", [128, 1], F32)
        rsv = sb("rsv", [128, 1], F32)
        vn = sb("vn", [128, 512], BF16)
        ssq = sb("ssq", [128, 8], F32)
        rsq = sb("rsq", [128, 8], F32)
        ssk = sb("ssk", [128, 8], F32)
        rsk = sb("rsk", [128, 8], F32)
        qn = sb("qn", [128, 512], BF16)
        kn = sb("kn", [128, 512], BF16)
        qT = sb("qT", [64, 8, 128], BF16)
        kTr = sb("kTr", [64, NSLOT * 8, 128], BF16)
        Vr = sb("Vr", [128, NSLOT * 8, 65], BF16)
        cat = xnb
        PTf = sb("PTf", [128, 640], F32)
        PTb = [sb("PTb%d" % i, [128, 640], BF16) for i in range(2)]
        rden = sb("rden", [128, 8], F32)
        XN2 = [sb("xn2_%d" % i, [128, D], F32) for i in range(2)]
        xn2b = xnb
        ss2 = sb("ss2", [128, 1], F32)
        rs2 = sb("rs2", [128, 1], F32)
        q2b = sb("q2b", [128, 2048], BF16)
        q2T = sb("q2T", [128, 2048], BF16)
        arenaA = sb("arenaA", [128, 2048], F32)
        arenaB = sb("arenaB", [128, 2048], F32)
        u_sb, qf, kf, sqt = (arenaA[:, i * 512:(i + 1) * 512] for i in range(4))
        gv = arenaB[:, 0:512]
        AA = ["aA0", "aA1", "aA2", "aA3"]
        AB = ["aB0", "aB1", "aB2", "aB3"]
        s_top = sb("s_top", [128, 256], F32)
        i_top = sb("i_top", [128, 256], U32)
        itf = sb("itf", [128, 256], F32)
        scr = sb("scr", [128, 128], F32)
        scr2 = sb("scr2", [128, 256], F32)
        best = sb("best", [128, 128], F32)
        pos = sb("pos", [128, 128], U32)
        pa = sb("pa", [128, 128], I32)
        pb = sb("pb", [128, 128], I32)
        paf = sb("paf", [128, 128], F32)
        pbf = sb("pbf", [128, 128], F32)
        e0 = sb("e0", [128, 128], F32)
        e1 = sb("e1", [128, 128], F32)
        EIDX = [sb("eidx%d" % i, [128, 128], I32) for i in range(2)]
        bm = sb("bm", [128, 128], F32)
        ge = sb("ge", [128, 128], F32)
        gs = sb("gs", [128, 8], F32)
        GN = [sb("gn%d" % i, [128, 128], F32) for i in range(2)]
        HPRE = [sb("hpre%d" % i, [128, 128], F32) for i in range(2)]
        hg = sb("hg", [128, 128], F32)
        tq = sb("tq", [128, 128], F32)
        xg = sb("xg", [128, 128], F32)
        WGT = [sb("wgt%d" % i, [128, 128], F32) for i in range(2)]

        psA = st.enter_context(nc.psum_tensor("psA", [128, 7 * 512], F32))
        psT = st.enter_context(nc.psum_tensor("psT", [128, 1024], BF16))

        def pg(c0, c1, plus=False):
            return ["B%d" % p + ("+" if plus else "") for p in range(c0 // 512, (c1 + 511) // 512)]

        def bankp(n, plus=False):
            return pg(n * 512, (n + 1) * 512, plus)

        def bank(n, w=512):
            return psA[:, n * 512:n * 512 + w]

        ACCC = 5 * 512

        S = Sched(nc)
        cnt = {"w": 0, "wp": 0, "g": 0, "d": 0}

        S.op("pool", I("iota", iotaf[:], pattern=[[1, 128]], base=0, channel_multiplier=0,
                       allow_small_or_imprecise_dtypes=True), writes=["iotaf"])
        S.op("pool", I("iota", pidx[:], pattern=[[0, 1]], base=0, channel_multiplier=1,
                       allow_small_or_imprecise_dtypes=True), writes=["pidx"])
        S.op("dve", I("tensor_scalar", out=identf[:], in0=iotaf[:], scalar1=pidx[:, 0:1], scalar2=None,
                      op0=ALU.is_equal), reads=["iotaf", "pidx"], writes=["identf"])
        S.op("dve", I("tensor_copy", out=identb[:], in_=identf[:]), reads=["identf"], writes=["identb"])
        S.op("pool", I("memset", Vr[:, :, 64:65], 1.0), writes=["V%d" % s for s in range(NSLOT)])

        for n in range(18):
            if n < 10:
                src = dap(win_h, n * 256, [[2560, 128], [128 * 2560, 8], [1, 256]])
            else:
                src = dap(wq_h, (n - 10) * 256, [[2048, 128], [128 * 2048, 8], [1, 256]])
            S.dma("pool", I("dma_start", out=wch_ap[n].rearrange("p (c k) -> p c k", c=8), in_=src),
                  "su_p", writes=["wch+"], group=True)
        for c in range(8):
            S.dma("pool", I("dma_start", out=wout[:, c, :], in_=wout_ap[c * 128:(c + 1) * 128, :]),
                  "su_p", writes=["wout+"], group=True)
        S.dma("sp", I("dma_start", out=g1b[:], in_=dap(g1_h, 0, [[0, 128], [1, D]])), "su_s", writes=["g1b"], group=True)
        S.dma("sp", I("dma_start", out=g2b[:], in_=dap(g2_h, 0, [[0, 128], [1, D]])), "su_s", writes=["g2b"], group=True)
        S.dma("sp", I("dma_start", out=ngf[:], in_=dap(ng_h, 0, [[0, 128], [1, 512]])), "su_s", writes=["ngf"], group=True)
        S.dma("sp", I("dma_start", out=qgb[:].rearrange("p (h d) -> p h d", h=8),
                      in_=dap(qg_h, 0, [[0, 128], [0, 8], [1, 64]])), "su_s", writes=["qgb"], group=True)
        S.dma("sp", I("dma_start", out=kgb[:].rearrange("p (h d) -> p h d", h=8),
                      in_=dap(kg_h, 0, [[0, 128], [0, 8], [1, 64]])), "su_s", writes=["kgb"], group=True)
        S.dma("sp", I("dma_start", out=bsT[:], in_=dap(bs_h, 0, [[1, 128], [128, 8]]),
                      allow_slow_non_contiguous=True), "su_s", writes=["bsT"], group=True)
        S.dma("sp", I("dma_start", out=c8[:], in_=dap(rb_h, 256, [[0, 128], [257, 8]]),
                      allow_slow_non_contiguous=True), "su_s", writes=["c8"], group=True)
        wsn = XN2[0]
        skn = arenaB
        S.dma("sp", I("dma_start", out=wsn[:].rearrange("p (g j) -> p g j", g=8),
                      in_=dap(ws_h, 0, [[128, 128], [16384, 8], [1, 128]])), "su_s", writes=["xn20"], group=True)
        S.dma("sp", I("dma_start", out=skn[:].rearrange("p (g j) -> p g j", g=16),
                      in_=dap(sk_h, 0, [[128, 128], [16384, 16], [1, 128]])), "su_s", writes=AB, group=True)
        rbD3 = rbD_h.ap().rearrange("p (h n) -> p h n", h=8)
        S.dma("sp", I("dma_start", out=rbD3[:, :, 0:257], in_=dap(rb_h, 0, [[0, 128], [257, 8], [1, 257]])),
              "su_s", writes=["rbD+"], group=True)
        S.op("act", I("activation", out=qgb[:], in_=qgb[:], func=AF.Copy, scale=0.125), reads=["qgb"], writes=["qgb"])
        S.op("dve", I("tensor_copy", out=bsf[:].rearrange("p (g c) -> p g c", g=8),
                      in_=bsT[:].unsqueeze(2).to_broadcast([128, 8, 64])), reads=["bsT"], writes=["bsf"])
        ext = XN2[1][:, 0:8 * 127].rearrange("p (h n) -> p h n", h=8)
        S.op("dve", I("tensor_copy", out=ext, in_=c8[:].unsqueeze(2).to_broadcast([128, 8, 127])),
             reads=["c8"], writes=["xn21"])
        S.dma("sp", I("dma_start", out=rbD3[:, :, 257:384], in_=ext), "su_rb1", reads=["xn21"], writes=["rbD+"])
        EBraw = arenaA[:].rearrange("p (h r i) -> p h r i", h=8, r=2)
        for r in range(2):
            S.dma("sp", I("dma_start", out=EBraw[:, :, r, :],
                          in_=dap(rbD_h, 128 + 128 * r, [[3071, 128], [384, 8], [1, 128]])),
                  "su_rb2", reads=["rbD"], writes=[a + "+" for a in AA], group=True)
        for r in range(2):
            S.op("act", I("activation", out=EB[:, :, r * 128:(r + 1) * 128], in_=EBraw[:, :, r, :], func=AF.Exp),
                 reads=AA, writes=["EB+"])
        S.op("act", I("activation", out=EB[:, :, 256:640], in_=c8[:].unsqueeze(2).to_broadcast([128, 8, 384]), func=AF.Exp),
             reads=["c8"], writes=["EB+"])
        S.op("dve", I("memset", EB[64:128, :, 0:64], 0.0), reads=["EB"], writes=["EB+"])
        S.op("dve", I("memset", EB[0:64, :, 512 + 64:640], 0.0), reads=["EB"], writes=["EB+"])
        S.op("dve", I("memset", wsn[0:64, :].rearrange("p (g j) -> p g j", g=8)[:, :, 64:128], 0.0),
             reads=["xn20"], writes=["xn20+"])
        for g in range(8):
            S.op("pe", I("transpose", out=psA[:, (8 + g) * 128:(9 + g) * 128], in_=wsn[:, g * 128:(g + 1) * 128], identity=identf[:]),
                 reads=["xn20", "identf"], writes=pg((8 + g) * 128, (9 + g) * 128, True))
        S.op("act", I("activation", out=wmT[:].rearrange("p g i -> p (g i)"), in_=psA[:, 1024:2048], func=AF.Copy),
             reads=pg(1024, 2048), writes=["wmT"])
        for half in range(2):
            for i in range(8):
                hp = half * 8 + i
                c0 = (0 if half == 0 else 2048) + i * 128
                S.op("pe", I("transpose", out=psA[:, c0:c0 + 128], in_=skn[:, hp * 128:(hp + 1) * 128], identity=identf[:]),
                     reads=AB + ["identf"], writes=pg(c0, c0 + 128, True))
            c0 = 0 if half == 0 else 2048
            S.op("act", I("activation", out=skT[:, half * 8:(half + 1) * 8, :].rearrange("p g i -> p (g i)"),
                          in_=psA[:, c0:c0 + 1024], func=AF.Copy),
                 reads=pg(c0, c0 + 1024), writes=["skT+"])
        RCH = 1024
        for (tab_ap, off) in ((pu_ap, 0), (pv_ap, D)):
            for r0 in range(0, 16384, RCH):
                S.dma("pool", I("dma_start", out=uv_ap[r0:r0 + RCH, off:off + D], in_=tab_ap[r0:r0 + RCH, :]),
                      "su_uv", writes=["uv+"], group=True)

        def rstd(ss_t, rs_t, nm, n_el):
            S.op("act", I("activation", out=rs_t[:], in_=ss_t[:], func=AF.Sqrt, scale=1.0 / n_el, bias=EPS),
                 reads=["ss" + nm], writes=["rs" + nm])
            S.op("dve", I("reciprocal", out=rs_t[:], in_=rs_t[:]), reads=["rs" + nm], writes=["rs" + nm])

        def transposes8(src, rsrc):
            for c in range(8):
                S.op("pe", I("transpose", out=psT[:, c * 128:(c + 1) * 128], in_=src[:, c * 128:(c + 1) * 128], identity=identb[:]),
                     reads=[rsrc, "identb"], writes=["BT" if c == 0 else "BT+"])
            S.op("act", I("activation", out=T8[:], in_=psT[:], func=AF.Copy), reads=["BT"], writes=["T8"])

        def top16(vals_ap, rvals, out_v, rv, out_i, ri, scratch, rscr, first):
            plus = "" if first else "+"
            S.op("dve", I("max", out=out_v[:, 0:8], in_=vals_ap), reads=[rvals], writes=[rv + plus])
            S.op("dve", I("match_replace", out=scratch, in_to_replace=out_v[:, 0:8], in_values=vals_ap, imm_value=-1e30),
                 reads=[rvals, rv], writes=[rscr])
            S.op("dve", I("max", out=out_v[:, 8:16], in_=scratch), reads=[rscr], writes=[rv + "+"])
            S.op("dve", I("max_index", out=out_i[:, 0:8], in_max=out_v[:, 0:8], in_values=vals_ap),
                 reads=[rvals, rv], writes=[ri + plus])
            S.op("dve", I("max_index", out=out_i[:, 8:16], in_max=out_v[:, 8:16], in_values=vals_ap),
                 reads=[rvals, rv], writes=[ri + "+"])

        def tr8(src, rsrc):
            for c in range(8):
                S.op("pe", I("transpose", out=psT[:, c * 128:(c + 1) * 128], in_=src[:, c * 128:(c + 1) * 128], identity=identb[:]),
                     reads=[rsrc, "identb"], writes=["BT" if c == 0 else "BT+"])
            yield 2
            S.op("act", I("activation", out=T8[:], in_=psT[:], func=AF.Copy), reads=["BT"], writes=["T8"])
            yield 2

        NCH = 18 * NT

        def prefetch():
            k = cnt["wp"]
            if k >= NCH:
                return
            cnt["wp"] += 1
            wb, rw = WBUF[k % NW], "wbuf%d" % (k % NW)
            S.dma("sp", I("dma_start", out=wb[:], in_=wch_ap[k % 18].rearrange("p (c k) -> p c k", c=8)),
                  "ldw%d" % (k % NW), reads=["wch"], writes=[rw])

        def stream_mm1(n, nbase):
            b = n - nbase
            for hh in range(2):
                k = cnt["w"]
                cnt["w"] += 1
                assert k % 18 == 2 * n + hh and k < cnt["wp"]
                wb, rw = WBUF[k % NW], "wbuf%d" % (k % NW)
                for c in range(8):
                    S.op("pe", I("matmul", out=psA[:, b * 512 + hh * 256:b * 512 + (hh + 1) * 256],
                                 lhsT=T8[:, c * 128:(c + 1) * 128], rhs=wb[:, c, :],
                                 start=(c == 0), stop=(c == 7)),
                         reads=["T8", rw], writes=bankp(b, not (c == 0 and hh == 0)))
                prefetch()

        def sqrt_(ss_t, rs_t, nm, n_el):
            S.op("act", I("activation", out=rs_t[:], in_=ss_t[:], func=AF.Ln, scale=1.0 / n_el, bias=EPS),
                 reads=["ss" + nm], writes=["rs" + nm])
            S.op("act", I("activation", out=rs_t[:], in_=rs_t[:], func=AF.Exp, scale=-0.5),
                 reads=["rs" + nm], writes=["rs" + nm])

        def recip_(rs_t, nm):
            pass

        def front(T):
            par = T & 1
            acc, racc = ACC[par], "acc%d" % par
            xn2, rxn2 = XN2[par], "xn2%d" % par
            slot = T % NSLOT
            S.dma("sp", I("dma_start", out=acc[:], in_=x_ap[T * 128:(T + 1) * 128, :]), "ldx%d" % par, writes=[racc])
            yield 1
            S.op("act", I("activation", out=xnb[:], in_=acc[:], func=AF.Square, accum_out=ss1[:]),
                 reads=[racc], writes=["xnb", "ss1"])
            sqrt_(ss1, rs1, "1", 1024)
            yield 1
            recip_(rs1, "1")
            S.op("dve", I("scalar_tensor_tensor", out=xnb[:], in0=acc[:], scalar=rs1[:, 0:1], in1=g1b[:],
                          op0=ALU.mult, op1=ALU.mult), reads=[racc, "rs1", "g1b"], writes=["xnb"])
            yield 1
            yield from tr8(xnb, "xnb")
            def evac_in(n):
                if n == 0:
                    S.op("act", I("activation", out=u_sb, in_=bank(0), func=AF.Gelu_apprx_tanh), reads=bankp(0), writes=["aA0"])
                elif n == 1:
                    S.op("act", I("activation", out=gv, in_=bank(1), func=AF.Gelu_apprx_tanh), reads=bankp(1), writes=["aB0"])
                elif n == 2:
                    S.op("act", I("activation", out=qf, in_=bank(2), func=AF.Copy), reads=bankp(2), writes=["aA1"])
                elif n == 3:
                    S.op("act", I("activation", out=kf, in_=bank(3), func=AF.Copy), reads=bankp(3), writes=["aA2"])
                else:
                    S.op("act", I("activation", out=Vr[:, slot * 8:(slot + 1) * 8, 0:64],
                                  in_=bank(4).rearrange("p (h d) -> p h d", h=8), func=AF.Copy),
                         reads=bankp(4), writes=["V%d+" % slot])

            for n in range(6):
                if n < 5:
                    stream_mm1(n, 0)
                    yield 3
                if n >= 1:
                    evac_in(n - 1)
                    yield 1
            S.op("dve", I("scalar_tensor_tensor", out=junkD[:, 0:512], in0=gv, scalar=1.0, in1=gv,
                          op0=ALU.mult, op1=ALU.mult, accum_out=ssv[:]), reads=["aB0"], writes=["junkD", "ssv"])
            h8 = "p (h d) -> p h d"
            S.op("dve", I("tensor_tensor", out=sqt, in0=qf, in1=qf, op=ALU.mult), reads=["aA1"], writes=["aA3"])
            S.op("dve", I("tensor_reduce", out=ssq[:], in_=sqt.rearrange(h8, h=8), axis=AX.X, op=ALU.add),
                 reads=["aA3"], writes=["ssq"])
            S.op("dve", I("tensor_tensor", out=sqt, in0=kf, in1=kf, op=ALU.mult), reads=["aA2"], writes=["aA3"])
            S.op("dve", I("tensor_reduce", out=ssk[:], in_=sqt.rearrange(h8, h=8), axis=AX.X, op=ALU.add),
                 reads=["aA3"], writes=["ssk"])
            yield 2
            sqrt_(ssv, rsv, "v", 512)
            sqrt_(ssq, rsq, "q", 64)
            sqrt_(ssk, rsk, "k", 64)
            yield 1
            recip_(rsv, "v")
            S.op("dve", I("scalar_tensor_tensor", out=vn[:], in0=gv, scalar=rsv[:, 0:1], in1=ngf[:],
                          op0=ALU.mult, op1=ALU.mult), reads=["aB0", "rsv", "ngf"], writes=["vn"])
            for (xf, rxf, rs_t, nm, gb, rgb, xo, rxo) in ((qf, "aA1", rsq, "q", qgb, "qgb", qn, "qn"),
                                                          (kf, "aA2", rsk, "k", kgb, "kgb", kn, "kn")):
                recip_(rs_t, nm)
                S.op("dve", I("tensor_tensor", out=sqt.rearrange(h8, h=8), in0=xf[:].rearrange(h8, h=8),
                              in1=rs_t[:].unsqueeze(2).to_broadcast([128, 8, 64]), op=ALU.mult),
                     reads=[rxf, "rs" + nm], writes=["aA3"])
                S.op("dve", I("tensor_tensor", out=xo[:], in0=sqt, in1=gb[:], op=ALU.mult),
                     reads=["aA3", rgb], writes=[rxo])
            yield 2
            for g in range(8):
                S.op("pe", I("matmul", out=psA[:, 4 * 512 + g * 64:4 * 512 + (g + 1) * 64], lhsT=wmT[:, g, :],
                             rhs=vn[:, g * 64:(g + 1) * 64], start=True, stop=True),
                     reads=["wmT", "vn"], writes=bankp(4, g != 0))
            for h in range(8):
                S.op("pe", I("transpose", out=psT[0:64, h * 128:(h + 1) * 128], in_=qn[:, h * 64:(h + 1) * 64], identity=identb[:]),
                     reads=["qn", "identb"], writes=["BT" if h == 0 else "BT+"])
            yield 1
            S.op("act", I("activation", out=qT[:].rearrange("p h t -> p (h t)"), in_=psT[0:64, :], func=AF.Copy),
                 reads=["BT"], writes=["qT"])
            S.op("dve", I("tensor_tensor", out=gv, in0=bank(4), in1=bsf[:], op=ALU.add),
                 reads=bankp(4) + ["bsf"], writes=["aB0"])
            S.op("dve", I("tensor_tensor", out=cat[:, 0:512], in0=gv, in1=u_sb, op=ALU.mult),
                 reads=["aB0", "aA0"], writes=["xnb"])
            yield 1
            for h in range(8):
                S.op("pe", I("transpose", out=psT[0:64, h * 128:(h + 1) * 128], in_=kn[:, h * 64:(h + 1) * 64], identity=identb[:]),
                     reads=["kn", "identb"], writes=["BT" if h == 0 else "BT+"])
            yield 1
            S.op("act", I("activation", out=kTr[:, slot * 8:(slot + 1) * 8, :].rearrange("p h t -> p (h t)"), in_=psT[0:64, :], func=AF.Copy),
                 reads=["BT"], writes=["kT%d" % slot])
            yield 1
            nr = min(5, T + 1)

            n4 = min(nr, 4)

            def s_cols(h, r):
                return (h % 2) * 512 + r * 128 if r < 4 else 1024 + (h % 2) * 128

            def scores(h):
                for r in range(nr):
                    sl = (T - r) % NSLOT
                    c0 = s_cols(h, r)
                    S.op("pe", I("matmul", out=psA[:, c0:c0 + 128],
                                 lhsT=kTr[:, sl * 8 + h, :], rhs=qT[:, h, :], start=True, stop=True),
                         reads=["kT%d" % sl, "qT"], writes=pg(c0, c0 + 128, r not in (0, 4)))

            scores(0)
            yield 1
            for h in range(8):
                c0 = s_cols(h, 0)
                S.op("act", I("activation", out=PTf[:, 0:n4 * 128], in_=psA[:, c0:c0 + n4 * 128], func=AF.Exp),
                     reads=pg(c0, c0 + n4 * 128), writes=["PTf"])
                if nr == 5:
                    c4 = s_cols(h, 4)
                    S.op("act", I("activation", out=PTf[:, 512:640], in_=psA[:, c4:c4 + 128], func=AF.Exp),
                         reads=pg(c4, c4 + 128), writes=["PTf+"])
                if h + 1 < 8:
                    scores(h + 1)
                yield 1
                pbuf, rpb = PTb[h % 2], "PTb%d" % (h % 2)
                S.op("dve", I("tensor_tensor", out=pbuf[:, 0:nr * 128], in0=PTf[:, 0:nr * 128], in1=EB[:, h, 0:nr * 128], op=ALU.mult),
                     reads=["PTf", "EB"], writes=[rpb])
                yield 1
                oc = (3 + h // 4) * 512 + (h % 4) * 128
                for r in range(nr):
                    sl = (T - r) % NSLOT
                    S.op("pe", I("matmul", out=psA[:, oc:oc + 65], lhsT=pbuf[:, r * 128:(r + 1) * 128],
                                 rhs=Vr[:, sl * 8 + h, :], start=(r == 0), stop=(r == nr - 1)),
                         reads=[rpb, "V%d" % sl], writes=pg(oc, oc + 128, not (r == 0 and h % 4 == 0)))
            yield 1
            for half in range(2):
                c0 = (3 + half) * 512
                bo = psA[:, c0:c0 + 512].rearrange("p (h c) -> p h c", h=4)
                S.op("dve", I("reciprocal", out=rden[:, half * 4:(half + 1) * 4].unsqueeze(2), in_=bo[:, :, 64:65]),
                     reads=pg(c0, c0 + 512), writes=["rden+"])
                S.op("dve", I("tensor_tensor", out=cat[:, 512 + half * 256:512 + (half + 1) * 256].rearrange("p (h d) -> p h d", h=4),
                              in0=bo[:, :, 0:64], in1=rden[:, half * 4:(half + 1) * 4].unsqueeze(2).to_broadcast([128, 4, 64]),
                              op=ALU.mult),
                     reads=pg(c0, c0 + 512) + ["rden"], writes=["xnb+"])
            yield 1
            yield from tr8(cat, "xnb")
            for n in range(2):
                for c in range(8):
                    S.op("pe", I("matmul", out=bank(n), lhsT=T8[:, c * 128:(c + 1) * 128], rhs=wout[:, c, n * 512:(n + 1) * 512],
                                 start=(c == 0), stop=(c == 7)),
                         reads=["T8", "wout"], writes=bankp(n, c != 0))
                yield 1
            S.op("dve", I("tensor_tensor", out=acc[:], in0=psA[:, 0:1024], in1=acc[:], op=ALU.add),
                 reads=pg(0, 1024) + [racc], writes=[racc])
            yield 1
            S.op("act", I("activation", out=xn2b[:], in_=acc[:], func=AF.Square, accum_out=ss2[:]),
                 reads=[racc], writes=["xnb", "ss2"])
            sqrt_(ss2, rs2, "2", 1024)
            yield 1
            recip_(rs2, "2")
            S.op("dve", I("scalar_tensor_tensor", out=xn2[:], in0=acc[:], scalar=rs2[:, 0:1], in1=g2b[:],
                          op0=ALU.mult, op1=ALU.mult), reads=[racc, "rs2", "g2b"], writes=[rxn2])
            yield 1
            S.op("act", I("activation", out=xn2b[:], in_=xn2[:], func=AF.Copy), reads=[rxn2], writes=["xnb"])
            yield 1
            yield from tr8(xn2b, "xnb")
            for n in range(5):
                if n < 4:
                    stream_mm1(5 + n, 5)
                    yield 3
                if n >= 1:
                    m = n - 1
                    S.op("act", I("activation", out=q2b[:, m * 512:(m + 1) * 512], in_=bank(m), func=AF.Copy),
                         reads=bankp(m), writes=["q2b" if m == 0 else "q2b+"])
                    yield 1
            for half in range(2):
                for i in range(8):
                    hp = half * 8 + i
                    S.op("pe", I("transpose", out=psT[:, i * 128:(i + 1) * 128], in_=q2b[:, hp * 128:(hp + 1) * 128], identity=identb[:]),
                         reads=["q2b", "identb"], writes=["BT" if i == 0 else "BT+"])
                yield 1
                S.op("act", I("activation", out=q2T[:, half * 1024:(half + 1) * 1024], in_=psT[:], func=AF.Copy),
                     reads=["BT"], writes=["q2T" if half == 0 else "q2T+"])
                yield 1
            sc = arenaB
            for n in range(4):
                for hp in range(4 * n, 4 * n + 4):
                    S.op("pe", I("matmul", out=psA[:, hp * 128:(hp + 1) * 128], lhsT=q2T[:, hp * 128:(hp + 1) * 128], rhs=skT[:, hp, :],
                                 start=True, stop=True),
                         reads=["q2T", "skT"], writes=pg(hp * 128, (hp + 1) * 128))
            yield 1
            for n in range(4):
                S.op("act", I("activation", out=sc[:, n * 512:(n + 1) * 512], in_=bank(n), func=AF.Copy),
                     reads=bankp(n), writes=["aB%d" % n])
                if n % 2 == 1:
                    yield 1
            for hp in range(16):
                top16(sc[:, hp * 128:(hp + 1) * 128], "aB%d" % (hp // 4), s_top[:, hp * 16:(hp + 1) * 16], "s_top",
                      i_top[:, hp * 16:(hp + 1) * 16], "i_top", scr[:], "scr", hp == 0)
                if hp % 2 == 1:
                    yield 3
            S.op("dve", I("tensor_copy", out=itf[:], in_=i_top[:]), reads=["i_top"], writes=["itf"])
            st4 = s_top[:].rearrange("p (h q k) -> p h q k", h=8, q=2)
            cand = arenaA
            S.op("dve", I("tensor_tensor", out=cand[:].rearrange("p (h a b) -> p h a b", h=8, a=16),
                          in0=st4[:, :, 0, :].unsqueeze(3).to_broadcast([128, 8, 16, 16]),
                          in1=st4[:, :, 1, :].unsqueeze(2).to_broadcast([128, 8, 16, 16]), op=ALU.add),
                 reads=["s_top"], writes=AA)
            yield 2
            for h in range(8):
                top16(cand[:, h * 256:(h + 1) * 256], "aA%d" % (h // 2), best[:, h * 16:(h + 1) * 16], "best",
                      pos[:, h * 16:(h + 1) * 16], "pos", scr2[:], "scr2", h == 0)
                if h % 2 == 1:
                    yield 4
            b3 = best[:].rearrange("p (h k) -> p h k", h=8)
            S.op("dve", I("tensor_tensor", out=bm[:].rearrange("p (h k) -> p h k", h=8), in0=b3,
                          in1=b3[:, :, 0:1].to_broadcast([128, 8, 16]), op=ALU.subtract), reads=["best"], writes=["bm"])
            S.op("act", I("activation", out=ge[:], in_=bm[:], func=AF.Exp), reads=["bm"], writes=["ge"])
            S.op("dve", I("tensor_single_scalar", out=pa[:], in_=pos[:].bitcast(I32), scalar=4, op=ALU.arith_shift_right),
                 reads=["pos"], writes=["pa"])
            S.op("dve", I("tensor_single_scalar", out=pb[:], in_=pos[:].bitcast(I32), scalar=15, op=ALU.bitwise_and),
                 reads=["pos"], writes=["pb"])
            S.op("dve", I("tensor_copy", out=paf[:], in_=pa[:]), reads=["pa"], writes=["paf"])
            S.op("dve", I("tensor_copy", out=pbf[:], in_=pb[:]), reads=["pb"], writes=["pbf"])
            yield 1
            E = arenaB
            E3 = E[:].rearrange("p (n a) -> p n a", a=16)
            E4 = E[:].rearrange("p (h k a) -> p h k a", h=8, k=16)
            it4 = itf[:].rearrange("p (h q a) -> p h q a", h=8, q=2)
            for (pf, rpf, q, eo, reo) in ((paf, "paf", 0, e0, "e0"), (pbf, "pbf", 1, e1, "e1")):
                S.op("dve", I("tensor_tensor", out=E3, in0=pf[:].unsqueeze(2).to_broadcast([128, 128, 16]),
                              in1=iotaf[:, 0:16].unsqueeze(1).to_broadcast([128, 128, 16]), op=ALU.is_equal),
                     reads=[rpf, "iotaf"], writes=AB)
                yield 2
                S.op("dve", I("tensor_tensor", out=E4, in0=E4, in1=it4[:, :, q, :].unsqueeze(2).to_broadcast([128, 8, 16, 16]),
                              op=ALU.mult), reads=AB + ["itf"], writes=AB)
                yield 2
                S.op("dve", I("tensor_reduce", out=eo[:], in_=E3, axis=AX.X, op=ALU.add), reads=AB, writes=[reo])
                yield 2
            S.op("dve", I("scalar_tensor_tensor", out=EIDX[par][:], in0=e0[:], scalar=128.0, in1=e1[:],
                          op0=ALU.mult, op1=ALU.add), reads=["e0", "e1"], writes=["eidx%d" % par])
            S.op("dve", I("tensor_reduce", out=gs[:], in_=ge[:].rearrange("p (h k) -> p h k", h=8), axis=AX.X, op=ALU.add),
                 reads=["ge"], writes=["gs"])
            S.op("dve", I("tensor_scalar", out=gs[:], in0=gs[:], scalar1=2.0, scalar2=None, op0=ALU.mult), reads=["gs"], writes=["gs"])
            S.op("dve", I("reciprocal", out=gs[:], in_=gs[:]), reads=["gs"], writes=["gs"])
            S.op("dve", I("tensor_tensor", out=GN[par][:].rearrange("p (h k) -> p h k", h=8),
                          in0=ge[:].rearrange("p (h k) -> p h k", h=8),
                          in1=gs[:].unsqueeze(2).to_broadcast([128, 8, 16]), op=ALU.mult),
                 reads=["ge", "gs"], writes=["gn%d" % par])
            yield 1

        def gather(eidx, reidx, j):
            k = cnt["g"]
            cnt["g"] += 1
            s = k % NG
            sem = "g%de%d" % (s, (k // NG) // 1000)
            S.dma("pool", I("indirect_dma_start", out=gbuf[:, s, :], out_offset=None, in_=uv_ap,
                            in_offset=bass.IndirectOffsetOnAxis(ap=eidx[:, j:j + 1], axis=0)),
                  sem, reads=[reidx, "uv"], writes=["gbuf%d" % s])
            return s

        GS = 4
        GK = 0.044715
        GC = 0.7978845608028654
        LAG_B1, LAG_B2, LAG_C = 1, 2, 3

        pend = {"add": None, "store": None}

        def finish_add():
            T = pend["add"]
            if T is None:
                return
            pend["add"] = None
            par = T & 1
            acc, racc = ACC[par], "acc%d" % par
            S.op("dve", I("tensor_tensor", out=acc[:], in0=psA[:, ACCC:ACCC + 1024], in1=acc[:], op=ALU.add),
                 reads=pg(ACCC, ACCC + 1024) + [racc], writes=[racc])
            pend["store"] = T

        def finish_store():
            T = pend["store"]
            if T is None:
                return
            pend["store"] = None
            par = T & 1
            S.dma("pool", I("dma_start", out=y_ap[T * 128:(T + 1) * 128, :], in_=ACC[par][:]), "st%d" % par,
                  reads=["acc%d" % par])

        def experts(T):
            par = T & 1
            acc, racc = ACC[par], "acc%d" % par
            xn2, rxn2 = XN2[par], "xn2%d" % par
            eidx, reidx = EIDX[par], "eidx%d" % par
            hpre, rh = HPRE[par], "hpre%d" % par
            wgt, rw = WGT[par], "wgt%d" % par
            rgn = "gn%d" % par
            slot_of = {}
            NGRP = 128 // GS
            assert GS - 1 + LAG_C > 4
            for j in range(128 + GS + LAG_C):
                if j == 4:
                    finish_add()
                if j == 10:
                    finish_store()
                if j < 128:
                    g = j // GS
                    rhg = "%sg%d" % (rh, g)
                    s_ = gather(eidx, reidx, j)
                    slot_of[j] = s_
                    S.op("dve", I("scalar_tensor_tensor", out=junkD[:], in0=gbuf[:, s_, 0:D], scalar=1.0, in1=xn2[:],
                                  op0=ALU.mult, op1=ALU.mult, accum_out=hpre[:, j:j + 1]),
                         reads=["gbuf%d" % s_, rxn2], writes=["junkD", rhg + "+"])
                ja = j - (GS - 1)
                if ja >= 0 and ja % GS == 0 and ja // GS < NGRP:
                    g = ja // GS
                    j0, j1 = g * GS, (g + 1) * GS
                    rhg = "%sg%d" % (rh, g)
                    hs = hpre[:, j0:j1]
                    S.op("act", I("activation", out=tq[:, j0:j1], in_=hs, func=AF.Square, scale=GK ** 0.5),
                         reads=[rhg], writes=["tq%d" % g])
                    S.op("dve", I("tensor_tensor", out=xg[:, j0:j1], in0=hs, in1=GN[par][:, j0:j1], op=ALU.mult),
                         reads=[rhg, rgn], writes=["xg%d" % g])
                jb = j - (GS - 1) - LAG_B1
                if jb >= 0 and jb % GS == 0 and jb // GS < NGRP:
                    g = jb // GS
                    j0, j1 = g * GS, (g + 1) * GS
                    rhg = "%sg%d" % (rh, g)
                    S.op("dve", I("scalar_tensor_tensor", out=tq[:, j0:j1], in0=tq[:, j0:j1], scalar=1.0, in1=hpre[:, j0:j1],
                                  op0=ALU.add, op1=ALU.mult), reads=["tq%d" % g, rhg], writes=["tq%d" % g])
                    S.op("act", I("activation", out=hg[:, j0:j1], in_=tq[:, j0:j1], func=AF.Tanh, scale=GC),
                         reads=["tq%d" % g], writes=["hg%d" % g])
                jb = j - (GS - 1) - LAG_B2
                if jb >= 0 and jb % GS == 0 and jb // GS < NGRP:
                    g = jb // GS
                    j0, j1 = g * GS, (g + 1) * GS
                    S.op("dve", I("scalar_tensor_tensor", out=wgt[:, j0:j1], in0=hg[:, j0:j1], scalar=1.0, in1=xg[:, j0:j1],
                                  op0=ALU.add, op1=ALU.mult), reads=["hg%d" % g, "xg%d" % g], writes=["%sg%d" % (rw, g)])
                q = j - (GS - 1) - LAG_C
                jcs = []
                if q >= 0 and q % GS < 2 and q // GS < NGRP:
                    jcs = [(q // GS) * GS + 2 * (q % GS), (q // GS) * GS + 2 * (q % GS) + 1]
                for jc in jcs:
                    s_ = slot_of.pop(jc)
                    rwg = "%sg%d" % (rw, jc // GS)
                    k = cnt["d"] % ND
                    cnt["d"] += 1
                    S.op("act", I("activation", out=DIAG[k][:], in_=identb[:], func=AF.Copy, scale=wgt[:, jc:jc + 1]),
                         reads=["identb", rwg], writes=["diag%d" % k])
                    for half in range(2):
                        c0 = ACCC + half * 512
                        S.op("pe", I("matmul", out=psA[:, c0:c0 + 512], lhsT=DIAG[k][:],
                                     rhs=gbuf[:, s_, D + half * 512:D + (half + 1) * 512],
                                     start=(jc == 0), stop=(jc == 127)),
                             reads=["diag%d" % k, "gbuf%d" % s_], writes=pg(c0, c0 + 512, jc != 0))
                yield
            assert not slot_of
            pend["add"] = T
            yield

        def drain(g):
            for _ in g:
                pass

        for _ in range(NW):
            prefetch()
        if not interleave:
            for T in range(NT):
                drain(front(T))
                drain(experts(T))
        else:
            units = sum(front(0))
            HEAD = 13
            total = 128 + GS + LAG_C + 1
            ratio = (total - HEAD) / float(units) * RATIO
            for T in range(NT):
                ge_ = experts(T)
                e_alive = True
                for _ in range(HEAD):
                    next(ge_)
                if T + 1 < NT:
                    credit = 0.0
                    for n in front(T + 1):
                        credit += n * ratio
                        while credit >= 1.0 and e_alive:
                            credit -= 1.0
                            try:
                                next(ge_)
                            except StopIteration:
                                e_alive = False
                if e_alive:
                    drain(ge_)

        finish_add()
        finish_store()
        toks = [(("dma", "st%d" % p), S.dma_count["st%d" % p]) for p in range(2) if ("st%d" % p) in S.dma_count]
        S.final_wait("pool", toks)
        S.emit()
    return nc


_PARAM_SHAPES = {
    "ln_mix_g": (1, D), "w_in": (D, 2560), "gmlp_norm_g": (1, 512), "gmlp_w_s": (1024, 128),
    "gmlp_b_s": (8, 128), "q_norm_g": (1, 64), "k_norm_g": (1, 64), "rel_bias": (8, 257),
    "w_out": (D, D), "ln_ffn_g": (1, D), "peer_w_query": (D, 2048), "peer_sub_keys": (2048, 128),
    "peer_u": (16384, D), "peer_v": (16384, D),
}


def _params(inputs):
    return {k: np.ascontiguousarray(np.asarray(inputs[k], dtype=np.float32).reshape(shp))
            for k, shp in _PARAM_SHAPES.items()}


def kernel(**inputs):
    x = np.ascontiguousarray(np.asarray(inputs["x"], dtype=np.float32))
    B = x.shape[0]
    common = _params(inputs)
    nc = build_nc(32)
    in_maps = [dict(common, x=x[b]) for b in range(B)]
    res = run_bass_kernel_spmd(nc, in_maps, core_ids=list(range(B)))
    return np.stack([np.asarray(r["y"]) for r in res.results], axis=0).astype(np.float32)
```

```python
import contextlib
import numpy as np
import concourse.bass as bass
import concourse.mybir as mybir
from concourse.bass_utils import run_bass_kernel_spmd

F32 = mybir.dt.float32
BF16 = mybir.dt.bfloat16
I32 = mybir.dt.int32
U32 = mybir.dt.uint32
ALU = mybir.AluOpType
AF = mybir.ActivationFunctionType
AX = mybir.AxisListType

EPOCH = 24000
EPS = 1e-6
NSLOT = 5
NG = 14
ND = 4
RATIO = 0.85
NW = 3
D = 1024


class Sched:
    ENGS = ("pe", "act", "dve", "pool", "sp")

    def __init__(self, nc):
        self.nc = nc
        self.stream = {e: [] for e in self.ENGS}
        self.count = {e: 0 for e in self.ENGS}
        self.waited = {e: {} for e in self.ENGS}
        self.writers = {}
        self.readers = {}
        self.dma_count = {}
        self.semkeys = []
        self.semset = set()
        self.group = set()
        self.sealed = set()

    def _key(self, k):
        if k not in self.semset:
            self.semset.add(k)
            self.semkeys.append(k)

    def _need(self, eng, tok, waits):
        key, val = tok
        if eng == "pe" and key[0] == "pe":
            return
        if key in self.group:
            val = self.dma_count[key[1]]
            self.sealed.add(key)
        if self.waited[eng].get(key, 0) < val:
            self.waited[eng][key] = val
            waits[key] = max(waits.get(key, 0), val)

    def _deps(self, eng, reads, writes, me=None):
        waits = {}
        for r in reads:
            for tok in self.writers.get(r, {}).values():
                self._need(eng, tok, waits)
        for w in writes:
            part = w.endswith("+")
            r = w[:-1] if part else w
            for weng, tok in self.writers.get(r, {}).items():
                if part and (weng == eng or weng == me):
                    continue
                self._need(eng, tok, waits)
            for tok in self.readers.get(r, {}).values():
                self._need(eng, tok, waits)
        return waits

    def _commit(self, eng, tok, reads, writes):
        for r in reads:
            self.readers.setdefault(r, {})[(eng, tok[0])] = tok
        for w in writes:
            part = w.endswith("+")
            r = w[:-1] if part else w
            if part:
                self.writers.setdefault(r, {})[eng] = tok
            else:
                self.writers[r] = {eng: tok}
            self.readers[r] = {}

    def op(self, eng, fn, reads=(), writes=()):
        waits = self._deps(eng, reads, writes)
        n = self.count[eng]
        self.count[eng] = n + 1
        key = (eng, n // EPOCH)
        self._key(key)
        tok = (key, n % EPOCH + 1)
        self.stream[eng].append((waits, fn, key, 1))
        self._commit(eng, tok, reads, writes)
        return tok

    def dma(self, eng, fn, sem, reads=(), writes=(), group=False):
        key = ("dma", sem)
        self._key(key)
        if group:
            self.group.add(key)
            assert key not in self.sealed, "group sem %s already waited on" % sem
        waits = self._deps(eng, reads, writes, me="dma:" + sem)
        v = self.dma_count.get(sem, 0) + 16
        self.dma_count[sem] = v
        tok = (key, v)
        self.stream[eng].append((waits, fn, key, 16))
        self._commit("dma:" + sem, tok, reads, writes)
        return tok

    def final_wait(self, eng, toks):
        waits = {}
        for tok in toks:
            self._need(eng, tok, waits)
        self.stream[eng].append((waits, None, None, 0))

    def emit(self):
        nc = self.nc
        with contextlib.ExitStack() as st:
            sems = {}
            for k in self.semkeys:
                sems[k] = st.enter_context(nc.semaphore("s_%s_%s" % (k[0], k[1])))
            block = st.enter_context(nc.Block())

            def run(engname):
                def body(e):
                    for waits, fn, key, inc in self.stream[engname]:
                        for wk, wv in waits.items():
                            e.wait_ge(sems[wk], wv)
                        if fn is not None:
                            fn(e).then_inc(sems[key], inc)
                return body

            block.tensor(run("pe"))
            block.scalar(run("act"))
            block.vector(run("dve"))
            block.gpsimd(run("pool"))
            block.sync(run("sp"))


def I(method, *a, **k):
    return lambda e: getattr(e, method)(*a, **k)


def build_nc(NT=32, interleave=True):
    nc = bass.Bass("TRN2", target_bir_lowering=False)
    NTOK = NT * 128

    def din(name, shape):
        return nc.dram_tensor(name, shape, F32, kind="ExternalInput")

    x_h = din("x", [NTOK, D])
    g1_h = din("ln_mix_g", [1, D])
    win_h = din("w_in", [D, 2560])
    ng_h = din("gmlp_norm_g", [1, 512])
    ws_h = din("gmlp_w_s", [1024, 128])
    bs_h = din("gmlp_b_s", [8, 128])
    qg_h = din("q_norm_g", [1, 64])
    kg_h = din("k_norm_g", [1, 64])
    rb_h = din("rel_bias", [8, 257])
    wout_h = din("w_out", [D, D])
    g2_h = din("ln_ffn_g", [1, D])
    wq_h = din("peer_w_query", [D, 2048])
    sk_h = din("peer_sub_keys", [2048, 128])
    pu_h = din("peer_u", [16384, D])
    pv_h = din("peer_v", [16384, D])
    y_h = nc.dram_tensor("y", [NTOK, D], F32, kind="ExternalOutput")
    wch_h = nc.dram_tensor("wch", [18, 128, 2048], BF16, kind="Internal")
    rbD_h = nc.dram_tensor("rbD", [128, 3072], F32, kind="Internal")
    uv_h = nc.dram_tensor("uv", [16384, 2 * D], BF16, kind="Internal")

    x_ap, y_ap = x_h.ap(), y_h.ap()
    wch_ap = wch_h.ap()
    pu_ap, pv_ap = pu_h.ap(), pv_h.ap()
    uv_ap = uv_h.ap()
    wout_ap = wout_h.ap()

    def dap(h, off, ap):
        return bass.AP(tensor=h, offset=off, ap=ap)

    with contextlib.ExitStack() as st:
        def sb(name, shape, dt):
            return st.enter_context(nc.sbuf_tensor(name, shape, dt))

        wout = sb("wout", [128, 8, D], BF16)
        skT = sb("skT", [128, 16, 128], BF16)
        wmT = sb("wmT", [128, 8, 128], BF16)
        EB = sb("EB", [128, 8, 640], BF16)
        g1b = sb("g1b", [128, D], F32)
        g2b = sb("g2b", [128, D], F32)
        ngf = sb("ngf", [128, 512], F32)
        bsf = sb("bsf", [128, 512], F32)
        bsT = sb("bsT", [128, 8], F32)
        c8 = sb("c8", [128, 8], F32)
        qgb = sb("qgb", [128, 512], F32)
        kgb = sb("kgb", [128, 512], F32)
        identf = sb("identf", [128, 128], F32)
        identb = sb("identb", [128, 128], BF16)
        iotaf = sb("iotaf", [128, 128], F32)
        pidx = sb("pidx", [128, 1], F32)
        WBUF = [sb("wbuf%d" % i, [128, 8, 256], BF16) for i in range(NW)]
        ACC = [sb("acc%d" % i, [128, D], F32) for i in range(2)]
        gbuf = sb("gbuf", [128, NG, 2 * D], BF16)
        DIAG = [sb("diag%d" % i, [128, 128], BF16) for i in range(ND)]
        junkD = sb("junkD", [128, D], BF16)
        ss1 = sb("ss1", [128, 1], F32)
        rs1 = sb("rs1", [128, 1], F32)
        xnb = sb("xnb", [128, D], BF16)
        T8 = sb("T8", [128, D], BF16)
        ssv = sb("ssv", [128, 1], F32)
        rsv = sb("rsv", [128, 1], F32)
        vn = sb("vn", [128, 512], BF16)
        ssq = sb("ssq", [128, 8], F32)
        rsq = sb("rsq", [128, 8], F32)
        ssk = sb("ssk", [128, 8], F32)
        rsk = sb("rsk", [128, 8], F32)
        qn = sb("qn", [128, 512], BF16)
        kn = sb("kn", [128, 512], BF16)
        qT = sb("qT", [64, 8, 128], BF16)
        kTr = sb("kTr", [64, NSLOT * 8, 128], BF16)
        Vr = sb("Vr", [128, NSLOT * 8, 65], BF16)
        cat = xnb
        PTf = sb("PTf", [128, 640], F32)
        PTb = [sb("PTb%d" % i, [128, 640], BF16) for i in range(2)]
        rden = sb("rden", [128, 8], F32)
        XN2 = [sb("xn2_%d" % i, [128, D], F32) for i in range(2)]
        xn2b = xnb
        ss2 = sb("ss2", [128, 1], F32)
        rs2 = sb("rs2", [128, 1], F32)
        q2b = sb("q2b", [128, 2048], BF16)
        q2T = sb("q2T", [128, 2048], BF16)
        arenaA = sb("arenaA", [128, 2048], F32)
        arenaB = sb("arenaB", [128, 2048], F32)
        u_sb, qf, kf, sqt = (arenaA[:, i * 512:(i + 1) * 512] for i in range(4))
        gv = arenaB[:, 0:512]
        AA = ["aA0", "aA1", "aA2", "aA3"]
        AB = ["aB0", "aB1", "aB2", "aB3"]
        s_top = sb("s_top", [128, 256], F32)
        i_top = sb("i_top", [128, 256], U32)
        itf = sb("itf", [128, 256], F32)
        scr = sb("scr", [128, 128], F32)
        scr2 = sb("scr2", [128, 256], F32)
        best = sb("best", [128, 128], F32)
        pos = sb("pos", [128, 128], U32)
        pa = sb("pa", [128, 128], I32)
        pb = sb("pb", [128, 128], I32)
        paf = sb("paf", [128, 128], F32)
        pbf = sb("pbf", [128, 128], F32)
        e0 = sb("e0", [128, 128], F32)
        e1 = sb("e1", [128, 128], F32)
        EIDX = [sb("eidx%d" % i, [128, 128], I32) for i in range(2)]
        bm = sb("bm", [128, 128], F32)
        ge = sb("ge", [128, 128], F32)
        gs = sb("gs", [128, 8], F32)
        GN = [sb("gn%d" % i, [128, 128], F32) for i in range(2)]
        HPRE = [sb("hpre%d" % i, [128, 128], F32) for i in range(2)]
        hg = sb("hg", [128, 128], F32)
        tq = sb("tq", [128, 128], F32)
        xg = sb("xg", [128, 128], F32)
        WGT = [sb("wgt%d" % i, [128, 128], F32) for i in range(2)]

        psA = st.enter_context(nc.psum_tensor("psA", [128, 7 * 512], F32))
        psT = st.enter_context(nc.psum_tensor("psT", [128, 1024], BF16))

        def pg(c0, c1, plus=False):
            return ["B%d" % p + ("+" if plus else "") for p in range(c0 // 512, (c1 + 511) // 512)]

        def bankp(n, plus=False):
            return pg(n * 512, (n + 1) * 512, plus)

        def bank(n, w=512):
            return psA[:, n * 512:n * 512 + w]

        ACCC = 5 * 512

        S = Sched(nc)
        cnt = {"w": 0, "wp": 0, "g": 0, "d": 0}

        S.op("pool", I("iota", iotaf[:], pattern=[[1, 128]], base=0, channel_multiplier=0,
                       allow_small_or_imprecise_dtypes=True), writes=["iotaf"])
        S.op("pool", I("iota", pidx[:], pattern=[[0, 1]], base=0, channel_multiplier=1,
                       allow_small_or_imprecise_dtypes=True), writes=["pidx"])
        S.op("dve", I("tensor_scalar", out=identf[:], in0=iotaf[:], scalar1=pidx[:, 0:1], scalar2=None,
                      op0=ALU.is_equal), reads=["iotaf", "pidx"], writes=["identf"])
        S.op("dve", I("tensor_copy", out=identb[:], in_=identf[:]), reads=["identf"], writes=["identb"])
        S.op("pool", I("memset", Vr[:, :, 64:65], 1.0), writes=["V%d" % s for s in range(NSLOT)])

        for n in range(18):
            if n < 10:
                src = dap(win_h, n * 256, [[2560, 128], [128 * 2560, 8], [1, 256]])
            else:
                src = dap(wq_h, (n - 10) * 256, [[2048, 128], [128 * 2048, 8], [1, 256]])
            S.dma("pool", I("dma_start", out=wch_ap[n].rearrange("p (c k) -> p c k", c=8), in_=src),
                  "su_p", writes=["wch+"], group=True)
        for c in range(8):
            S.dma("pool", I("dma_start", out=wout[:, c, :], in_=wout_ap[c * 128:(c + 1) * 128, :]),
                  "su_p", writes=["wout+"], group=True)
        S.dma("sp", I("dma_start", out=g1b[:], in_=dap(g1_h, 0, [[0, 128], [1, D]])), "su_s", writes=["g1b"], group=True)
        S.dma("sp", I("dma_start", out=g2b[:], in_=dap(g2_h, 0, [[0, 128], [1, D]])), "su_s", writes=["g2b"], group=True)
        S.dma("sp", I("dma_start", out=ngf[:], in_=dap(ng_h, 0, [[0, 128], [1, 512]])), "su_s", writes=["ngf"], group=True)
        S.dma("sp", I("dma_start", out=qgb[:].rearrange("p (h d) -> p h d", h=8),
                      in_=dap(qg_h, 0, [[0, 128], [0, 8], [1, 64]])), "su_s", writes=["qgb"], group=True)
        S.dma("sp", I("dma_start", out=kgb[:].rearrange("p (h d) -> p h d", h=8),
                      in_=dap(kg_h, 0, [[0, 128], [0, 8], [1, 64]])), "su_s", writes=["kgb"], group=True)
        S.dma("sp", I("dma_start", out=bsT[:], in_=dap(bs_h, 0, [[1, 128], [128, 8]]),
                      allow_slow_non_contiguous=True), "su_s", writes=["bsT"], group=True)
        S.dma("sp", I("dma_start", out=c8[:], in_=dap(rb_h, 256, [[0, 128], [257, 8]]),
                      allow_slow_non_contiguous=True), "su_s", writes=["c8"], group=True)
        wsn = XN2[0]
        skn = arenaB
        S.dma("sp", I("dma_start", out=wsn[:].rearrange("p (g j) -> p g j", g=8),
                      in_=dap(ws_h, 0, [[128, 128], [16384, 8], [1, 128]])), "su_s", writes=["xn20"], group=True)
        S.dma("sp", I("dma_start", out=skn[:].rearrange("p (g j) -> p g j", g=16),
                      in_=dap(sk_h, 0, [[128, 128], [16384, 16], [1, 128]])), "su_s", writes=AB, group=True)
        rbD3 = rbD_h.ap().rearrange("p (h n) -> p h n", h=8)
        S.dma("sp", I("dma_start", out=rbD3[:, :, 0:257], in_=dap(rb_h, 0, [[0, 128], [257, 8], [1, 257]])),
              "su_s", writes=["rbD+"], group=True)
        S.op("act", I("activation", out=qgb[:], in_=qgb[:], func=AF.Copy, scale=0.125), reads=["qgb"], writes=["qgb"])
        S.op("dve", I("tensor_copy", out=bsf[:].rearrange("p (g c) -> p g c", g=8),
                      in_=bsT[:].unsqueeze(2).to_broadcast([128, 8, 64])), reads=["bsT"], writes=["bsf"])
        ext = XN2[1][:, 0:8 * 127].rearrange("p (h n) -> p h n", h=8)
        S.op("dve", I("tensor_copy", out=ext, in_=c8[:].unsqueeze(2).to_broadcast([128, 8, 127])),
             reads=["c8"], writes=["xn21"])
        S.dma("sp", I("dma_start", out=rbD3[:, :, 257:384], in_=ext), "su_rb1", reads=["xn21"], writes=["rbD+"])
        EBraw = arenaA[:].rearrange("p (h r i) -> p h r i", h=8, r=2)
        for r in range(2):
            S.dma("sp", I("dma_start", out=EBraw[:, :, r, :],
                          in_=dap(rbD_h, 128 + 128 * r, [[3071, 128], [384, 8], [1, 128]])),
                  "su_rb2", reads=["rbD"], writes=[a + "+" for a in AA], group=True)
        for r in range(2):
            S.op("act", I("activation", out=EB[:, :, r * 128:(r + 1) * 128], in_=EBraw[:, :, r, :], func=AF.Exp),
                 reads=AA, writes=["EB+"])
        S.op("act", I("activation", out=EB[:, :, 256:640], in_=c8[:].unsqueeze(2).to_broadcast([128, 8, 384]), func=AF.Exp),
             reads=["c8"], writes=["EB+"])
        S.op("dve", I("memset", EB[64:128, :, 0:64], 0.0), reads=["EB"], writes=["EB+"])
        S.op("dve", I("memset", EB[0:64, :, 512 + 64:640], 0.0), reads=["EB"], writes=["EB+"])
        S.op("dve", I("memset", wsn[0:64, :].rearrange("p (g j) -> p g j", g=8)[:, :, 64:128], 0.0),
             reads=["xn20"], writes=["xn20+"])
        for g in range(8):
            S.op("pe", I("transpose", out=psA[:, (8 + g) * 128:(9 + g) * 128], in_=wsn[:, g * 128:(g + 1) * 128], identity=identf[:]),
                 reads=["xn20", "identf"], writes=pg((8 + g) * 128, (9 + g) * 128, True))
        S.op("act", I("activation", out=wmT[:].rearrange("p g i -> p (g i)"), in_=psA[:, 1024:2048], func=AF.Copy),
             reads=pg(1024, 2048), writes=["wmT"])
        for half in range(2):
            for i in range(8):
                hp = half * 8 + i
                c0 = (0 if half == 0 else 2048) + i * 128
                S.op("pe", I("transpose", out=psA[:, c0:c0 + 128], in_=skn[:, hp * 128:(hp + 1) * 128], identity=identf[:]),
                     reads=AB + ["identf"], writes=pg(c0, c0 + 128, True))
            c0 = 0 if half == 0 else 2048
            S.op("act", I("activation", out=skT[:, half * 8:(half + 1) * 8, :].rearrange("p g i -> p (g i)"),
                          in_=psA[:, c0:c0 + 1024], func=AF.Copy),
                 reads=pg(c0, c0 + 1024), writes=["skT+"])
        RCH = 1024
        for (tab_ap, off) in ((pu_ap, 0), (pv_ap, D)):
            for r0 in range(0, 16384, RCH):
                S.dma("pool", I("dma_start", out=uv_ap[r0:r0 + RCH, off:off + D], in_=tab_ap[r0:r0 + RCH, :]),
                      "su_uv", writes=["uv+"], group=True)

        def rstd(ss_t, rs_t, nm, n_el):
            S.op("act", I("activation", out=rs_t[:], in_=ss_t[:], func=AF.Sqrt, scale=1.0 / n_el, bias=EPS),
                 reads=["ss" + nm], writes=["rs" + nm])
            S.op("dve", I("reciprocal", out=rs_t[:], in_=rs_t[:]), reads=["rs" + nm], writes=["rs" + nm])

        def transposes8(src, rsrc):
            for c in range(8):
                S.op("pe", I("transpose", out=psT[:, c * 128:(c + 1) * 128], in_=src[:, c * 128:(c + 1) * 128], identity=identb[:]),
                     reads=[rsrc, "identb"], writes=["BT" if c == 0 else "BT+"])
            S.op("act", I("activation", out=T8[:], in_=psT[:], func=AF.Copy), reads=["BT"], writes=["T8"])

        def top16(vals_ap, rvals, out_v, rv, out_i, ri, scratch, rscr, first):
            plus = "" if first else "+"
            S.op("dve", I("max", out=out_v[:, 0:8], in_=vals_ap), reads=[rvals], writes=[rv + plus])
            S.op("dve", I("match_replace", out=scratch, in_to_replace=out_v[:, 0:8], in_values=vals_ap, imm_value=-1e30),
                 reads=[rvals, rv], writes=[rscr])
            S.op("dve", I("max", out=out_v[:, 8:16], in_=scratch), reads=[rscr], writes=[rv + "+"])
            S.op("dve", I("max_index", out=out_i[:, 0:8], in_max=out_v[:, 0:8], in_values=vals_ap),
                 reads=[rvals, rv], writes=[ri + plus])
            S.op("dve", I("max_index", out=out_i[:, 8:16], in_max=out_v[:, 8:16], in_values=vals_ap),
                 reads=[rvals, rv], writes=[ri + "+"])

        def tr8(src, rsrc):
            for c in range(8):
                S.op("pe", I("transpose", out=psT[:, c * 128:(c + 1) * 128], in_=src[:, c * 128:(c + 1) * 128], identity=identb[:]),
                     reads=[rsrc, "identb"], writes=["BT" if c == 0 else "BT+"])
            yield 2
            S.op("act", I("activation", out=T8[:], in_=psT[:], func=AF.Copy), reads=["BT"], writes=["T8"])
            yield 2

        NCH = 18 * NT

        def prefetch():
            k = cnt["wp"]
            if k >= NCH:
                return
            cnt["wp"] += 1
            wb, rw = WBUF[k % NW], "wbuf%d" % (k % NW)
            S.dma("sp", I("dma_start", out=wb[:], in_=wch_ap[k % 18].rearrange("p (c k) -> p c k", c=8)),
                  "ldw%d" % (k % NW), reads=["wch"], writes=[rw])

        def stream_mm1(n, nbase):
            b = n - nbase
            for hh in range(2):
                k = cnt["w"]
                cnt["w"] += 1
                assert k % 18 == 2 * n + hh and k < cnt["wp"]
                wb, rw = WBUF[k % NW], "wbuf%d" % (k % NW)
                for c in range(8):
                    S.op("pe", I("matmul", out=psA[:, b * 512 + hh * 256:b * 512 + (hh + 1) * 256],
                                 lhsT=T8[:, c * 128:(c + 1) * 128], rhs=wb[:, c, :],
                                 start=(c == 0), stop=(c == 7)),
                         reads=["T8", rw], writes=bankp(b, not (c == 0 and hh == 0)))
                prefetch()

        def sqrt_(ss_t, rs_t, nm, n_el):
            S.op("act", I("activation", out=rs_t[:], in_=ss_t[:], func=AF.Ln, scale=1.0 / n_el, bias=EPS),
                 reads=["ss" + nm], writes=["rs" + nm])
            S.op("act", I("activation", out=rs_t[:], in_=rs_t[:], func=AF.Exp, scale=-0.5),
                 reads=["rs" + nm], writes=["rs" + nm])

        def recip_(rs_t, nm):
            pass

        def front(T):
            par = T & 1
            acc, racc = ACC[par], "acc%d" % par
            xn2, rxn2 = XN2[par], "xn2%d" % par
            slot = T % NSLOT
            S.dma("sp", I("dma_start", out=acc[:], in_=x_ap[T * 128:(T + 1) * 128, :]), "ldx%d" % par, writes=[racc])
            yield 1
            S.op("act", I("activation", out=xnb[:], in_=acc[:], func=AF.Square, accum_out=ss1[:]),
                 reads=[racc], writes=["xnb", "ss1"])
            sqrt_(ss1, rs1, "1", 1024)
            yield 1
            recip_(rs1, "1")
            S.op("dve", I("scalar_tensor_tensor", out=xnb[:], in0=acc[:], scalar=rs1[:, 0:1], in1=g1b[:],
                          op0=ALU.mult, op1=ALU.mult), reads=[racc, "rs1", "g1b"], writes=["xnb"])
            yield 1
            yield from tr8(xnb, "xnb")
            def evac_in(n):
                if n == 0:
                    S.op("act", I("activation", out=u_sb, in_=bank(0), func=AF.Gelu_apprx_tanh), reads=bankp(0), writes=["aA0"])
                elif n == 1:
                    S.op("act", I("activation", out=gv, in_=bank(1), func=AF.Gelu_apprx_tanh), reads=bankp(1), writes=["aB0"])
                elif n == 2:
                    S.op("act", I("activation", out=qf, in_=bank(2), func=AF.Copy), reads=bankp(2), writes=["aA1"])
                elif n == 3:
                    S.op("act", I("activation", out=kf, in_=bank(3), func=AF.Copy), reads=bankp(3), writes=["aA2"])
                else:
                    S.op("act", I("activation", out=Vr[:, slot * 8:(slot + 1) * 8, 0:64],
                                  in_=bank(4).rearrange("p (h d) -> p h d", h=8), func=AF.Copy),
                         reads=bankp(4), writes=["V%d+" % slot])

            for n in range(6):
                if n < 5:
                    stream_mm1(n, 0)
                    yield 3
                if n >= 1:
                    evac_in(n - 1)
                    yield 1
            S.op("dve", I("scalar_tensor_tensor", out=junkD[:, 0:512], in0=gv, scalar=1.0, in1=gv,
                          op0=ALU.mult, op1=ALU.mult, accum_out=ssv[:]), reads=["aB0"], writes=["junkD", "ssv"])
            h8 = "p (h d) -> p h d"
            S.op("dve", I("tensor_tensor", out=sqt, in0=qf, in1=qf, op=ALU.mult), reads=["aA1"], writes=["aA3"])
            S.op("dve", I("tensor_reduce", out=ssq[:], in_=sqt.rearrange(h8, h=8), axis=AX.X, op=ALU.add),
                 reads=["aA3"], writes=["ssq"])
            S.op("dve", I("tensor_tensor", out=sqt, in0=kf, in1=kf, op=ALU.mult), reads=["aA2"], writes=["aA3"])
            S.op("dve", I("tensor_reduce", out=ssk[:], in_=sqt.rearrange(h8, h=8), axis=AX.X, op=ALU.add),
                 reads=["aA3"], writes=["ssk"])
            yield 2
            sqrt_(ssv, rsv, "v", 512)
            sqrt_(ssq, rsq, "q", 64)
            sqrt_(ssk, rsk, "k", 64)
            yield 1
            recip_(rsv, "v")
            S.op("dve", I("scalar_tensor_tensor", out=vn[:], in0=gv, scalar=rsv[:, 0:1], in1=ngf[:],
                          op0=ALU.mult, op1=ALU.mult), reads=["aB0", "rsv", "ngf"], writes=["vn"])
            for (xf, rxf, rs_t, nm, gb, rgb, xo, rxo) in ((qf, "aA1", rsq, "q", qgb, "qgb", qn, "qn"),
                                                          (kf, "aA2", rsk, "k", kgb, "kgb", kn, "kn")):
                recip_(rs_t, nm)
                S.op("dve", I("tensor_tensor", out=sqt.rearrange(h8, h=8), in0=xf[:].rearrange(h8, h=8),
                              in1=rs_t[:].unsqueeze(2).to_broadcast([128, 8, 64]), op=ALU.mult),
                     reads=[rxf, "rs" + nm], writes=["aA3"])
                S.op("dve", I("tensor_tensor", out=xo[:], in0=sqt, in1=gb[:], op=ALU.mult),
                     reads=["aA3", rgb], writes=[rxo])
            yield 2
            for g in range(8):
                S.op("pe", I("matmul", out=psA[:, 4 * 512 + g * 64:4 * 512 + (g + 1) * 64], lhsT=wmT[:, g, :],
                             rhs=vn[:, g * 64:(g + 1) * 64], start=True, stop=True),
                     reads=["wmT", "vn"], writes=bankp(4, g != 0))
            for h in range(8):
                S.op("pe", I("transpose", out=psT[0:64, h * 128:(h + 1) * 128], in_=qn[:, h * 64:(h + 1) * 64], identity=identb[:]),
                     reads=["qn", "identb"], writes=["BT" if h == 0 else "BT+"])
            yield 1
            S.op("act", I("activation", out=qT[:].rearrange("p h t -> p (h t)"), in_=psT[0:64, :], func=AF.Copy),
                 reads=["BT"], writes=["qT"])
            S.op("dve", I("tensor_tensor", out=gv, in0=bank(4), in1=bsf[:], op=ALU.add),
                 reads=bankp(4) + ["bsf"], writes=["aB0"])
            S.op("dve", I("tensor_tensor", out=cat[:, 0:512], in0=gv, in1=u_sb, op=ALU.mult),
                 reads=["aB0", "aA0"], writes=["xnb"])
            yield 1
            for h in range(8):
                S.op("pe", I("transpose", out=psT[0:64, h * 128:(h + 1) * 128], in_=kn[:, h * 64:(h + 1) * 64], identity=identb[:]),
                     reads=["kn", "identb"], writes=["BT" if h == 0 else "BT+"])
            yield 1
            S.op("act", I("activation", out=kTr[:, slot * 8:(slot + 1) * 8, :].rearrange("p h t -> p (h t)"), in_=psT[0:64, :], func=AF.Copy),
                 reads=["BT"], writes=["kT%d" % slot])
            yield 1
            nr = min(5, T + 1)

            n4 = min(nr, 4)

            def s_cols(h, r):
                return (h % 2) * 512 + r * 128 if r < 4 else 1024 + (h % 2) * 128

            def scores(h):
                for r in range(nr):
                    sl = (T - r) % NSLOT
                    c0 = s_cols(h, r)
                    S.op("pe", I("matmul", out=psA[:, c0:c0 + 128],
                                 lhsT=kTr[:, sl * 8 + h, :], rhs=qT[:, h, :], start=True, stop=True),
                         reads=["kT%d" % sl, "qT"], writes=pg(c0, c0 + 128, r not in (0, 4)))

            scores(0)
            yield 1
            for h in range(8):
                c0 = s_cols(h, 0)
                S.op("act", I("activation", out=PTf[:, 0:n4 * 128], in_=psA[:, c0:c0 + n4 * 128], func=AF.Exp),
                     reads=pg(c0, c0 + n4 * 128), writes=["PTf"])
                if nr == 5:
                    c4 = s_cols(h, 4)
                    S.op("act", I("activation", out=PTf[:, 512:640], in_=psA[:, c4:c4 + 128], func=AF.Exp),
                         reads=pg(c4, c4 + 128), writes=["PTf+"])
                if h + 1 < 8:
                    scores(h + 1)
                yield 1
                pbuf, rpb = PTb[h % 2], "PTb%d" % (h % 2)
                S.op("dve", I("tensor_tensor", out=pbuf[:, 0:nr * 128], in0=PTf[:, 0:nr * 128], in1=EB[:, h, 0:nr * 128], op=ALU.mult),
                     reads=["PTf", "EB"], writes=[rpb])
                yield 1
                oc = (3 + h // 4) * 512 + (h % 4) * 128
                for r in range(nr):
                    sl = (T - r) % NSLOT
                    S.op("pe", I("matmul", out=psA[:, oc:oc + 65], lhsT=pbuf[:, r * 128:(r + 1) * 128],
                                 rhs=Vr[:, sl * 8 + h, :], start=(r == 0), stop=(r == nr - 1)),
                         reads=[rpb, "V%d" % sl], writes=pg(oc, oc + 128, not (r == 0 and h % 4 == 0)))
            yield 1
            for half in range(2):
                c0 = (3 + half) * 512
                bo = psA[:, c0:c0 + 512].rearrange("p (h c) -> p h c", h=4)
                S.op("dve", I("reciprocal", out=rden[:, half * 4:(half + 1) * 4].unsqueeze(2), in_=bo[:, :, 64:65]),
                     reads=pg(c0, c0 + 512), writes=["rden+"])
                S.op("dve", I("tensor_tensor", out=cat[:, 512 + half * 256:512 + (half + 1) * 256].rearrange("p (h d) -> p h d", h=4),
                              in0=bo[:, :, 0:64], in1=rden[:, half * 4:(half + 1) * 4].unsqueeze(2).to_broadcast([128, 4, 64]),
                              op=ALU.mult),
                     reads=pg(c0, c0 + 512) + ["rden"], writes=["xnb+"])
            yield 1
            yield from tr8(cat, "xnb")
            for n in range(2):
                for c in range(8):
                    S.op("pe", I("matmul", out=bank(n), lhsT=T8[:, c * 128:(c + 1) * 128], rhs=wout[:, c, n * 512:(n + 1) * 512],
                                 start=(c == 0), stop=(c == 7)),
                         reads=["T8", "wout"], writes=bankp(n, c != 0))
                yield 1
            S.op("dve", I("tensor_tensor", out=acc[:], in0=psA[:, 0:1024], in1=acc[:], op=ALU.add),
                 reads=pg(0, 1024) + [racc], writes=[racc])
            yield 1
            S.op("act", I("activation", out=xn2b[:], in_=acc[:], func=AF.Square, accum_out=ss2[:]),
                 reads=[racc], writes=["xnb", "ss2"])
            sqrt_(ss2, rs2, "2", 1024)
            yield 1
            recip_(rs2, "2")
            S.op("dve", I("scalar_tensor_tensor", out=xn2[:], in0=acc[:], scalar=rs2[:, 0:1], in1=g2b[:],
                          op0=ALU.mult, op1=ALU.mult), reads=[racc, "rs2", "g2b"], writes=[rxn2])
            yield 1
            S.op("act", I("activation", out=xn2b[:], in_=xn2[:], func=AF.Copy), reads=[rxn2], writes=["xnb"])
            yield 1
            yield from tr8(xn2b, "xnb")
            for n in range(5):
                if n < 4:
                    stream_mm1(5 + n, 5)
                    yield 3
                if n >= 1:
                    m = n - 1
                    S.op("act", I("activation", out=q2b[:, m * 512:(m + 1) * 512], in_=bank(m), func=AF.Copy),
                         reads=bankp(m), writes=["q2b" if m == 0 else "q2b+"])
                    yield 1
            for half in range(2):
                for i in range(8):
                    hp = half * 8 + i
                    S.op("pe", I("transpose", out=psT[:, i * 128:(i + 1) * 128], in_=q2b[:, hp * 128:(hp + 1) * 128], identity=identb[:]),
                         reads=["q2b", "identb"], writes=["BT" if i == 0 else "BT+"])
                yield 1
                S.op("act", I("activation", out=q2T[:, half * 1024:(half + 1) * 1024], in_=psT[:], func=AF.Copy),
                     reads=["BT"], writes=["q2T" if half == 0 else "q2T+"])
                yield 1
            sc = arenaB
            for n in range(4):
                for hp in range(4 * n, 4 * n + 4):
                    S.op("pe", I("matmul", out=psA[:, hp * 128:(hp + 1) * 128], lhsT=q2T[:, hp * 128:(hp + 1) * 128], rhs=skT[:, hp, :],
                                 start=True, stop=True),
                         reads=["q2T", "skT"], writes=pg(hp * 128, (hp + 1) * 128))
            yield 1
            for n in range(4):
                S.op("act", I("activation", out=sc[:, n * 512:(n + 1) * 512], in_=bank(n), func=AF.Copy),
                     reads=bankp(n), writes=["aB%d" % n])
                if n % 2 == 1:
                    yield 1
            for hp in range(16):
                top16(sc[:, hp * 128:(hp + 1) * 128], "aB%d" % (hp // 4), s_top[:, hp * 16:(hp + 1) * 16], "s_top",
                      i_top[:, hp * 16:(hp + 1) * 16], "i_top", scr[:], "scr", hp == 0)
                if hp % 2 == 1:
                    yield 3
            S.op("dve", I("tensor_copy", out=itf[:], in_=i_top[:]), reads=["i_top"], writes=["itf"])
            st4 = s_top[:].rearrange("p (h q k) -> p h q k", h=8, q=2)
            cand = arenaA
            S.op("dve", I("tensor_tensor", out=cand[:].rearrange("p (h a b) -> p h a b", h=8, a=16),
                          in0=st4[:, :, 0, :].unsqueeze(3).to_broadcast([128, 8, 16, 16]),
                          in1=st4[:, :, 1, :].unsqueeze(2).to_broadcast([128, 8, 16, 16]), op=ALU.add),
                 reads=["s_top"], writes=AA)
            yield 2
            for h in range(8):
                top16(cand[:, h * 256:(h + 1) * 256], "aA%d" % (h // 2), best[:, h * 16:(h + 1) * 16], "best",
                      pos[:, h * 16:(h + 1) * 16], "pos", scr2[:], "scr2", h == 0)
                if h % 2 == 1:
                    yield 4
            b3 = best[:].rearrange("p (h k) -> p h k", h=8)
            S.op("dve", I("tensor_tensor", out=bm[:].rearrange("p (h k) -> p h k", h=8), in0=b3,
                          in1=b3[:, :, 0:1].to_broadcast([128, 8, 16]), op=ALU.subtract), reads=["best"], writes=["bm"])
            S.op("act", I("activation", out=ge[:], in_=bm[:], func=AF.Exp), reads=["bm"], writes=["ge"])
            S.op("dve", I("tensor_single_scalar", out=pa[:], in_=pos[:].bitcast(I32), scalar=4, op=ALU.arith_shift_right),
                 reads=["pos"], writes=["pa"])
            S.op("dve", I("tensor_single_scalar", out=pb[:], in_=pos[:].bitcast(I32), scalar=15, op=ALU.bitwise_and),
                 reads=["pos"], writes=["pb"])
            S.op("dve", I("tensor_copy", out=paf[:], in_=pa[:]), reads=["pa"], writes=["paf"])
            S.op("dve", I("tensor_copy", out=pbf[:], in_=pb[:]), reads=["pb"], writes=["pbf"])
            yield 1
            E = arenaB
            E3 = E[:].rearrange("p (n a) -> p n a", a=16)
            E4 = E[:].rearrange("p (h k a) -> p h k a", h=8, k=16)
            it4 = itf[:].rearrange("p (h q a) -> p h q a", h=8, q=2)
            for (pf, rpf, q, eo, reo) in ((paf, "paf", 0, e0, "e0"), (pbf, "pbf", 1, e1, "e1")):
                S.op("dve", I("tensor_tensor", out=E3, in0=pf[:].unsqueeze(2).to_broadcast([128, 128, 16]),
                              in1=iotaf[:, 0:16].unsqueeze(1).to_broadcast([128, 128, 16]), op=ALU.is_equal),
                     reads=[rpf, "iotaf"], writes=AB)
                yield 2
                S.op("dve", I("tensor_tensor", out=E4, in0=E4, in1=it4[:, :, q, :].unsqueeze(2).to_broadcast([128, 8, 16, 16]),
                              op=ALU.mult), reads=AB + ["itf"], writes=AB)
                yield 2
                S.op("dve", I("tensor_reduce", out=eo[:], in_=E3, axis=AX.X, op=ALU.add), reads=AB, writes=[reo])
                yield 2
            S.op("dve", I("scalar_tensor_tensor", out=EIDX[par][:], in0=e0[:], scalar=128.0, in1=e1[:],
                          op0=ALU.mult, op1=ALU.add), reads=["e0", "e1"], writes=["eidx%d" % par])
            S.op("dve", I("tensor_reduce", out=gs[:], in_=ge[:].rearrange("p (h k) -> p h k", h=8), axis=AX.X, op=ALU.add),
                 reads=["ge"], writes=["gs"])
            S.op("dve", I("tensor_scalar", out=gs[:], in0=gs[:], scalar1=2.0, scalar2=None, op0=ALU.mult), reads=["gs"], writes=["gs"])
            S.op("dve", I("reciprocal", out=gs[:], in_=gs[:]), reads=["gs"], writes=["gs"])
            S.op("dve", I("tensor_tensor", out=GN[par][:].rearrange("p (h k) -> p h k", h=8),
                          in0=ge[:].rearrange("p (h k) -> p h k", h=8),
                          in1=gs[:].unsqueeze(2).to_broadcast([128, 8, 16]), op=ALU.mult),
                 reads=["ge", "gs"], writes=["gn%d" % par])
            yield 1

        def gather(eidx, reidx, j):
            k = cnt["g"]
            cnt["g"] += 1
            s = k % NG
            sem = "g%de%d" % (s, (k // NG) // 1000)
            S.dma("pool", I("indirect_dma_start", out=gbuf[:, s, :], out_offset=None, in_=uv_ap,
                            in_offset=bass.IndirectOffsetOnAxis(ap=eidx[:, j:j + 1], axis=0)),
                  sem, reads=[reidx, "uv"], writes=["gbuf%d" % s])
            return s

        GS = 4
        GK = 0.044715
        GC = 0.7978845608028654
        LAG_B1, LAG_B2, LAG_C = 1, 2, 2

        pend = {"add": None, "store": None}

        def finish_add():
            T = pend["add"]
            if T is None:
                return
            pend["add"] = None
            par = T & 1
            acc, racc = ACC[par], "acc%d" % par
            S.op("dve", I("tensor_tensor", out=acc[:], in0=psA[:, ACCC:ACCC + 1024], in1=acc[:], op=ALU.add),
                 reads=pg(ACCC, ACCC + 1024) + [racc], writes=[racc])
            pend["store"] = T

        def finish_store():
            T = pend["store"]
            if T is None:
                return
            pend["store"] = None
            par = T & 1
            S.dma("pool", I("dma_start", out=y_ap[T * 128:(T + 1) * 128, :], in_=ACC[par][:]), "st%d" % par,
                  reads=["acc%d" % par])

        def experts(T):
            par = T & 1
            acc, racc = ACC[par], "acc%d" % par
            xn2, rxn2 = XN2[par], "xn2%d" % par
            eidx, reidx = EIDX[par], "eidx%d" % par
            hpre, rh = HPRE[par], "hpre%d" % par
            wgt, rw = WGT[par], "wgt%d" % par
            rgn = "gn%d" % par
            slot_of = {}
            NGRP = 128 // GS
            assert GS - 1 + LAG_C > 4
            for j in range(128 + GS + LAG_C):
                if j == 4:
                    finish_add()
                if j == 10:
                    finish_store()
                if j < 128:
                    g = j // GS
                    rhg = "%sg%d" % (rh, g)
                    s_ = gather(eidx, reidx, j)
                    slot_of[j] = s_
                    S.op("dve", I("scalar_tensor_tensor", out=junkD[:], in0=gbuf[:, s_, 0:D], scalar=1.0, in1=xn2[:],
                                  op0=ALU.mult, op1=ALU.mult, accum_out=hpre[:, j:j + 1]),
                         reads=["gbuf%d" % s_, rxn2], writes=["junkD", rhg + "+"])
                ja = j - (GS - 1)
                if ja >= 0 and ja % GS == 0 and ja // GS < NGRP:
                    g = ja // GS
                    j0, j1 = g * GS, (g + 1) * GS
                    rhg = "%sg%d" % (rh, g)
                    hs = hpre[:, j0:j1]
                    S.op("act", I("activation", out=tq[:, j0:j1], in_=hs, func=AF.Square, scale=GK ** 0.5),
                         reads=[rhg], writes=["tq%d" % g])
                    S.op("dve", I("tensor_tensor", out=xg[:, j0:j1], in0=hs, in1=GN[par][:, j0:j1], op=ALU.mult),
                         reads=[rhg, rgn], writes=["xg%d" % g])
                jb = j - (GS - 1) - LAG_B1
                if jb >= 0 and jb % GS == 0 and jb // GS < NGRP:
                    g = jb // GS
                    j0, j1 = g * GS, (g + 1) * GS
                    rhg = "%sg%d" % (rh, g)
                    S.op("dve", I("scalar_tensor_tensor", out=tq[:, j0:j1], in0=tq[:, j0:j1], scalar=1.0, in1=hpre[:, j0:j1],
                                  op0=ALU.add, op1=ALU.mult), reads=["tq%d" % g, rhg], writes=["tq%d" % g])
                    S.op("act", I("activation", out=hg[:, j0:j1], in_=tq[:, j0:j1], func=AF.Tanh, scale=GC),
                         reads=["tq%d" % g], writes=["hg%d" % g])
                jb = j - (GS - 1) - LAG_B2
                if jb >= 0 and jb % GS == 0 and jb // GS < NGRP:
                    g = jb // GS
                    j0, j1 = g * GS, (g + 1) * GS
                    S.op("dve", I("scalar_tensor_tensor", out=wgt[:, j0:j1], in0=hg[:, j0:j1], scalar=1.0, in1=xg[:, j0:j1],
                                  op0=ALU.add, op1=ALU.mult), reads=["hg%d" % g, "xg%d" % g], writes=["%sg%d" % (rw, g)])
                q = j - (GS - 1) - LAG_C
                jcs = []
                if q >= 0 and q % GS < 2 and q // GS < NGRP:
                    jcs = [(q // GS) * GS + 2 * (q % GS), (q // GS) * GS + 2 * (q % GS) + 1]
                for jc in jcs:
                    s_ = slot_of.pop(jc)
                    rwg = "%sg%d" % (rw, jc // GS)
                    k = cnt["d"] % ND
                    cnt["d"] += 1
                    S.op("act", I("activation", out=DIAG[k][:], in_=identb[:], func=AF.Copy, scale=wgt[:, jc:jc + 1]),
                         reads=["identb", rwg], writes=["diag%d" % k])
                    for half in range(2):
                        c0 = ACCC + half * 512
                        S.op("pe", I("matmul", out=psA[:, c0:c0 + 512], lhsT=DIAG[k][:],
                                     rhs=gbuf[:, s_, D + half * 512:D + (half + 1) * 512],
                                     start=(jc == 0), stop=(jc == 127)),
                             reads=["diag%d" % k, "gbuf%d" % s_], writes=pg(c0, c0 + 512, jc != 0))
                yield
            assert not slot_of
            pend["add"] = T
            yield

        def drain(g):
            for _ in g:
                pass

        for _ in range(NW):
            prefetch()
        if not interleave:
            for T in range(NT):
                drain(front(T))
                drain(experts(T))
        else:
            units = sum(front(0))
            HEAD = 13
            total = 128 + GS + LAG_C + 1
            ratio = (total - HEAD) / float(units) * RATIO
            for T in range(NT):
                ge_ = experts(T)
                e_alive = True
                for _ in range(HEAD):
                    next(ge_)
                if T + 1 < NT:
                    credit = 0.0
                    for n in front(T + 1):
                        credit += n * ratio
                        while credit >= 1.0 and e_alive:
                            credit -= 1.0
                            try:
                                next(ge_)
                            except StopIteration:
                                e_alive = False
                if e_alive:
                    drain(ge_)

        finish_add()
        finish_store()
        toks = [(("dma", "st%d" % p), S.dma_count["st%d" % p]) for p in range(2) if ("st%d" % p) in S.dma_count]
        S.final_wait("pool", toks)
        S.emit()
    return nc


_PARAM_SHAPES = {
    "ln_mix_g": (1, D), "w_in": (D, 2560), "gmlp_norm_g": (1, 512), "gmlp_w_s": (1024, 128),
    "gmlp_b_s": (8, 128), "q_norm_g": (1, 64), "k_norm_g": (1, 64), "rel_bias": (8, 257),
    "w_out": (D, D), "ln_ffn_g": (1, D), "peer_w_query": (D, 2048), "peer_sub_keys": (2048, 128),
    "peer_u": (16384, D), "peer_v": (16384, D),
}


def _params(inputs):
    return {k: np.ascontiguousarray(np.asarray(inputs[k], dtype=np.float32).reshape(shp))
            for k, shp in _PARAM_SHAPES.items()}


def kernel(**inputs):
    x = np.ascontiguousarray(np.asarray(inputs["x"], dtype=np.float32))
    B = x.shape[0]
    common = _params(inputs)
    nc = build_nc(32)
    in_maps = [dict(common, x=x[b]) for b in range(B)]
    res = run_bass_kernel_spmd(nc, in_maps, core_ids=list(range(B)))
    return np.stack([np.asarray(r["y"]) for r in res.results], axis=0).astype(np.float32)
```

```python
import contextlib
import numpy as np
import concourse.bass as bass
import concourse.mybir as mybir
from concourse.bass_utils import run_bass_kernel_spmd

F32 = mybir.dt.float32
BF16 = mybir.dt.bfloat16
I32 = mybir.dt.int32
U32 = mybir.dt.uint32
ALU = mybir.AluOpType
AF = mybir.ActivationFunctionType
AX = mybir.AxisListType

EPOCH = 24000
EPS = 1e-6
NSLOT = 5
NG = 14
ND = 4
RATIO = 0.85
NW = 3
D = 1024


class Sched:
    ENGS = ("pe", "act", "dve", "pool", "sp")

    def __init__(self, nc):
        self.nc = nc
        self.stream = {e: [] for e in self.ENGS}
        self.count = {e: 0 for e in self.ENGS}
        self.waited = {e: {} for e in self.ENGS}
        self.writers = {}
        self.readers = {}
        self.dma_count = {}
        self.semkeys = []
        self.semset = set()
        self.group = set()
        self.sealed = set()

    def _key(self, k):
        if k not in self.semset:
            self.semset.add(k)
            self.semkeys.append(k)

    def _need(self, eng, tok, waits):
        key, val = tok
        if eng == "pe" and key[0] == "pe":
            return
        if key in self.group:
            val = self.dma_count[key[1]]
            self.sealed.add(key)
        if self.waited[eng].get(key, 0) < val:
            self.waited[eng][key] = val
            waits[key] = max(waits.get(key, 0), val)

    def _deps(self, eng, reads, writes, me=None):
        waits = {}
        for r in reads:
            for tok in self.writers.get(r, {}).values():
                self._need(eng, tok, waits)
        for w in writes:
            part = w.endswith("+")
            r = w[:-1] if part else w
            for weng, tok in self.writers.get(r, {}).items():
                if part and (weng == eng or weng == me):
                    continue
                self._need(eng, tok, waits)
            for tok in self.readers.get(r, {}).values():
                self._need(eng, tok, waits)
        return waits

    def _commit(self, eng, tok, reads, writes):
        for r in reads:
            self.readers.setdefault(r, {})[(eng, tok[0])] = tok
        for w in writes:
            part = w.endswith("+")
            r = w[:-1] if part else w
            if part:
                self.writers.setdefault(r, {})[eng] = tok
            else:
                self.writers[r] = {eng: tok}
            self.readers[r] = {}

    def op(self, eng, fn, reads=(), writes=()):
        waits = self._deps(eng, reads, writes)
        n = self.count[eng]
        self.count[eng] = n + 1
        key = (eng, n // EPOCH)
        self._key(key)
        tok = (key, n % EPOCH + 1)
        self.stream[eng].append((waits, fn, key, 1))
        self._commit(eng, tok, reads, writes)
        return tok

    def dma(self, eng, fn, sem, reads=(), writes=(), group=False):
        key = ("dma", sem)
        self._key(key)
        if group:
            self.group.add(key)
            assert key not in self.sealed, "group sem %s already waited on" % sem
        waits = self._deps(eng, reads, writes, me="dma:" + sem)
        v = self.dma_count.get(sem, 0) + 16
        self.dma_count[sem] = v
        tok = (key, v)
        self.stream[eng].append((waits, fn, key, 16))
        self._commit("dma:" + sem, tok, reads, writes)
        return tok

    def final_wait(self, eng, toks):
        waits = {}
        for tok in toks:
            self._need(eng, tok, waits)
        self.stream[eng].append((waits, None, None, 0))

    def emit(self):
        nc = self.nc
        with contextlib.ExitStack() as st:
            sems = {}
            for k in self.semkeys:
                sems[k] = st.enter_context(nc.semaphore("s_%s_%s" % (k[0], k[1])))
            block = st.enter_context(nc.Block())

            def run(engname):
                def body(e):
                    for waits, fn, key, inc in self.stream[engname]:
                        for wk, wv in waits.items():
                            e.wait_ge(sems[wk], wv)
                        if fn is not None:
                            fn(e).then_inc(sems[key], inc)
                return body

            block.tensor(run("pe"))
            block.scalar(run("act"))
            block.vector(run("dve"))
            block.gpsimd(run("pool"))
            block.sync(run("sp"))


def I(method, *a, **k):
    return lambda e: getattr(e, method)(*a, **k)


def build_nc(NT=32, interleave=True):
    nc = bass.Bass("TRN2", target_bir_lowering=False)
    NTOK = NT * 128

    def din(name, shape):
        return nc.dram_tensor(name, shape, F32, kind="ExternalInput")

    x_h = din("x", [NTOK, D])
    g1_h = din("ln_mix_g", [1, D])
    win_h = din("w_in", [D, 2560])
    ng_h = din("gmlp_norm_g", [1, 512])
    ws_h = din("gmlp_w_s", [1024, 128])
    bs_h = din("gmlp_b_s", [8, 128])
    qg_h = din("q_norm_g", [1, 64])
    kg_h = din("k_norm_g", [1, 64])
    rb_h = din("rel_bias", [8, 257])
    wout_h = din("w_out", [D, D])
    g2_h = din("ln_ffn_g", [1, D])
    wq_h = din("peer_w_query", [D, 2048])
    sk_h = din("peer_sub_keys", [2048, 128])
    pu_h = din("peer_u", [16384, D])
    pv_h = din("peer_v", [16384, D])
    y_h = nc.dram_tensor("y", [NTOK, D], F32, kind="ExternalOutput")
    wch_h = nc.dram_tensor("wch", [18, 128, 2048], BF16, kind="Internal")
    rbD_h = nc.dram_tensor("rbD", [128, 3072], F32, kind="Internal")
    uv_h = nc.dram_tensor("uv", [16384, 2 * D], BF16, kind="Internal")

    x_ap, y_ap = x_h.ap(), y_h.ap()
    wch_ap = wch_h.ap()
    pu_ap, pv_ap = pu_h.ap(), pv_h.ap()
    uv_ap = uv_h.ap()
    wout_ap = wout_h.ap()

    def dap(h, off, ap):
        return bass.AP(tensor=h, offset=off, ap=ap)

    with contextlib.ExitStack() as st:
        def sb(name, shape, dt):
            return st.enter_context(nc.sbuf_tensor(name, shape, dt))

        wout = sb("wout", [128, 8, D], BF16)
        skT = sb("skT", [128, 16, 128], BF16)
        wmT = sb("wmT", [128, 8, 128], BF16)
        EB = sb("EB", [128, 8, 640], BF16)
        g1b = sb("g1b", [128, D], F32)
        g2b = sb("g2b", [128, D], F32)
        ngf = sb("ngf", [128, 512], F32)
        bsf = sb("bsf", [128, 512], F32)
        bsT = sb("bsT", [128, 8], F32)
        c8 = sb("c8", [128, 8], F32)
        qgb = sb("qgb", [128, 512], F32)
        kgb = sb("kgb", [128, 512], F32)
        identf = sb("identf", [128, 128], F32)
        identb = sb("identb", [128, 128], BF16)
        iotaf = sb("iotaf", [128, 128], F32)
        pidx = sb("pidx", [128, 1], F32)
        WBUF = [sb("wbuf%d" % i, [128, 8, 256], BF16) for i in range(NW)]
        ACC = [sb("acc%d" % i, [128, D], F32) for i in range(2)]
        gbuf = sb("gbuf", [128, NG, 2 * D], BF16)
        DIAG = [sb("diag%d" % i, [128, 128], BF16) for i in range(ND)]
        junkD = sb("junkD", [128, D], BF16)
        ss1 = sb("ss1", [128, 1], F32)
        rs1 = sb("rs1", [128, 1], F32)
        xnb = sb("xnb", [128, D], BF16)
        T8 = sb("T8", [128, D], BF16)
        ssv = sb("ssv", [128, 1], F32)
        rsv = sb("rsv", [128, 1], F32)
        vn = sb("vn", [128, 512], BF16)
        ssq = sb("ssq", [128, 8], F32)
        rsq = sb("rsq", [128, 8], F32)
        ssk = sb("ssk", [128, 8], F32)
        rsk = sb("rsk", [128, 8], F32)
        qn = sb("qn", [128, 512], BF16)
        kn = sb("kn", [128, 512], BF16)
        qT = sb("qT", [64, 8, 128], BF16)
        kTr = sb("kTr", [64, NSLOT * 8, 128], BF16)
        Vr = sb("Vr", [128, NSLOT * 8, 65], BF16)
        cat = xnb
        PTf = sb("PTf", [128, 640], F32)
        PTb = [sb("PTb%d" % i, [128, 640], BF16) for i in range(2)]
        rden = sb("rden", [128, 8], F32)
        XN2 = [sb("xn2_%d" % i, [128, D], F32) for i in range(2)]
        xn2b = xnb
        ss2 = sb("ss2", [128, 1], F32)
        rs2 = sb("rs2", [128, 1], F32)
        q2b = sb("q2b", [128, 2048], BF16)
        q2T = sb("q2T", [128, 2048], BF16)
        arenaA = sb("arenaA", [128, 2048], F32)
        arenaB = sb("arenaB", [128, 2048], F32)
        u_sb, qf, kf, sqt = (arenaA[:, i * 512:(i + 1) * 512] for i in range(4))
        gv = arenaB[:, 0:512]
        AA = ["aA0", "aA1", "aA2", "aA3"]
        AB = ["aB0", "aB1", "aB2", "aB3"]
        s_top = sb("s_top", [128, 256], F32)
        i_top = sb("i_top", [128, 256], U32)
        itf = sb("itf", [128, 256], F32)
        scr = sb("scr", [128, 128], F32)
        scr2 = sb("scr2", [128, 256], F32)
        scrb = sb("scrb", [128, 128], F32)
        scr2b = sb("scr2b", [128, 256], F32)
        best = sb("best", [128, 128], F32)
        pos = sb("pos", [128, 128], U32)
        pa = sb("pa", [128, 128], I32)
        pb = sb("pb", [128, 128], I32)
        paf = sb("paf", [128, 128], F32)
        pbf = sb("pbf", [128, 128], F32)
        e0 = sb("e0", [128, 128], F32)
        e1 = sb("e1", [128, 128], F32)
        EIDX = [sb("eidx%d" % i, [128, 128], I32) for i in range(2)]
        bm = sb("bm", [128, 128], F32)
        ge = sb("ge", [128, 128], F32)
        gs = sb("gs", [128, 8], F32)
        GN = [sb("gn%d" % i, [128, 128], F32) for i in range(2)]
        HPRE = [sb("hpre%d" % i, [128, 128], F32) for i in range(2)]
        hg = sb("hg", [128, 128], F32)
        tq = sb("tq", [128, 128], F32)
        xg = sb("xg", [128, 128], F32)
        WGT = [sb("wgt%d" % i, [128, 128], F32) for i in range(2)]

        psA = st.enter_context(nc.psum_tensor("psA", [128, 7 * 512], F32))
        psT = st.enter_context(nc.psum_tensor("psT", [128, 1024], BF16))

        def pg(c0, c1, plus=False):
            return ["B%d" % p + ("+" if plus else "") for p in range(c0 // 512, (c1 + 511) // 512)]

        def bankp(n, plus=False):
            return pg(n * 512, (n + 1) * 512, plus)

        def bank(n, w=512):
            return psA[:, n * 512:n * 512 + w]

        ACCC = 5 * 512

        S = Sched(nc)
        cnt = {"w": 0, "wp": 0, "g": 0, "d": 0}

        S.op("pool", I("iota", iotaf[:], pattern=[[1, 128]], base=0, channel_multiplier=0,
                       allow_small_or_imprecise_dtypes=True), writes=["iotaf"])
        S.op("pool", I("iota", pidx[:], pattern=[[0, 1]], base=0, channel_multiplier=1,
                       allow_small_or_imprecise_dtypes=True), writes=["pidx"])
        S.op("dve", I("tensor_scalar", out=identf[:], in0=iotaf[:], scalar1=pidx[:, 0:1], scalar2=None,
                      op0=ALU.is_equal), reads=["iotaf", "pidx"], writes=["identf"])
        S.op("dve", I("tensor_copy", out=identb[:], in_=identf[:]), reads=["identf"], writes=["identb"])
        S.op("pool", I("memset", Vr[:, :, 64:65], 1.0), writes=["V%d" % s for s in range(NSLOT)])

        for n in range(18):
            if n < 10:
                src = dap(win_h, n * 256, [[2560, 128], [128 * 2560, 8], [1, 256]])
            else:
                src = dap(wq_h, (n - 10) * 256, [[2048, 128], [128 * 2048, 8], [1, 256]])
            S.dma("pool", I("dma_start", out=wch_ap[n].rearrange("p (c k) -> p c k", c=8), in_=src),
                  "su_p", writes=["wch+"], group=True)
        for c in range(8):
            S.dma("pool", I("dma_start", out=wout[:, c, :], in_=wout_ap[c * 128:(c + 1) * 128, :]),
                  "su_p", writes=["wout+"], group=True)
        S.dma("sp", I("dma_start", out=g1b[:], in_=dap(g1_h, 0, [[0, 128], [1, D]])), "su_s", writes=["g1b"], group=True)
        S.dma("sp", I("dma_start", out=g2b[:], in_=dap(g2_h, 0, [[0, 128], [1, D]])), "su_s", writes=["g2b"], group=True)
        S.dma("sp", I("dma_start", out=ngf[:], in_=dap(ng_h, 0, [[0, 128], [1, 512]])), "su_s", writes=["ngf"], group=True)
        S.dma("sp", I("dma_start", out=qgb[:].rearrange("p (h d) -> p h d", h=8),
                      in_=dap(qg_h, 0, [[0, 128], [0, 8], [1, 64]])), "su_s", writes=["qgb"], group=True)
        S.dma("sp", I("dma_start", out=kgb[:].rearrange("p (h d) -> p h d", h=8),
                      in_=dap(kg_h, 0, [[0, 128], [0, 8], [1, 64]])), "su_s", writes=["kgb"], group=True)
        S.dma("sp", I("dma_start", out=bsT[:], in_=dap(bs_h, 0, [[1, 128], [128, 8]]),
                      allow_slow_non_contiguous=True), "su_s", writes=["bsT"], group=True)
        S.dma("sp", I("dma_start", out=c8[:], in_=dap(rb_h, 256, [[0, 128], [257, 8]]),
                      allow_slow_non_contiguous=True), "su_s", writes=["c8"], group=True)
        wsn = XN2[0]
        skn = arenaB
        S.dma("sp", I("dma_start", out=wsn[:].rearrange("p (g j) -> p g j", g=8),
                      in_=dap(ws_h, 0, [[128, 128], [16384, 8], [1, 128]])), "su_s", writes=["xn20"], group=True)
        S.dma("sp", I("dma_start", out=skn[:].rearrange("p (g j) -> p g j", g=16),
                      in_=dap(sk_h, 0, [[128, 128], [16384, 16], [1, 128]])), "su_s", writes=AB, group=True)
        rbD3 = rbD_h.ap().rearrange("p (h n) -> p h n", h=8)
        S.dma("sp", I("dma_start", out=rbD3[:, :, 0:257], in_=dap(rb_h, 0, [[0, 128], [257, 8], [1, 257]])),
              "su_s", writes=["rbD+"], group=True)
        S.op("act", I("activation", out=qgb[:], in_=qgb[:], func=AF.Copy, scale=0.125), reads=["qgb"], writes=["qgb"])
        S.op("dve", I("tensor_copy", out=bsf[:].rearrange("p (g c) -> p g c", g=8),
                      in_=bsT[:].unsqueeze(2).to_broadcast([128, 8, 64])), reads=["bsT"], writes=["bsf"])
        ext = XN2[1][:, 0:8 * 127].rearrange("p (h n) -> p h n", h=8)
        S.op("dve", I("tensor_copy", out=ext, in_=c8[:].unsqueeze(2).to_broadcast([128, 8, 127])),
             reads=["c8"], writes=["xn21"])
        S.dma("sp", I("dma_start", out=rbD3[:, :, 257:384], in_=ext), "su_rb1", reads=["xn21"], writes=["rbD+"])
        EBraw = arenaA[:].rearrange("p (h r i) -> p h r i", h=8, r=2)
        for r in range(2):
            S.dma("sp", I("dma_start", out=EBraw[:, :, r, :],
                          in_=dap(rbD_h, 128 + 128 * r, [[3071, 128], [384, 8], [1, 128]])),
                  "su_rb2", reads=["rbD"], writes=[a + "+" for a in AA], group=True)
        for r in range(2):
            S.op("act", I("activation", out=EB[:, :, r * 128:(r + 1) * 128], in_=EBraw[:, :, r, :], func=AF.Exp),
                 reads=AA, writes=["EB+"])
        S.op("act", I("activation", out=EB[:, :, 256:640], in_=c8[:].unsqueeze(2).to_broadcast([128, 8, 384]), func=AF.Exp),
             reads=["c8"], writes=["EB+"])
        S.op("dve", I("memset", EB[64:128, :, 0:64], 0.0), reads=["EB"], writes=["EB+"])
        S.op("dve", I("memset", EB[0:64, :, 512 + 64:640], 0.0), reads=["EB"], writes=["EB+"])
        S.op("dve", I("memset", wsn[0:64, :].rearrange("p (g j) -> p g j", g=8)[:, :, 64:128], 0.0),
             reads=["xn20"], writes=["xn20+"])
        for g in range(8):
            S.op("pe", I("transpose", out=psA[:, (8 + g) * 128:(9 + g) * 128], in_=wsn[:, g * 128:(g + 1) * 128], identity=identf[:]),
                 reads=["xn20", "identf"], writes=pg((8 + g) * 128, (9 + g) * 128, True))
        S.op("act", I("activation", out=wmT[:].rearrange("p g i -> p (g i)"), in_=psA[:, 1024:2048], func=AF.Copy),
             reads=pg(1024, 2048), writes=["wmT"])
        for half in range(2):
            for i in range(8):
                hp = half * 8 + i
                c0 = (0 if half == 0 else 2048) + i * 128
                S.op("pe", I("transpose", out=psA[:, c0:c0 + 128], in_=skn[:, hp * 128:(hp + 1) * 128], identity=identf[:]),
                     reads=AB + ["identf"], writes=pg(c0, c0 + 128, True))
            c0 = 0 if half == 0 else 2048
            S.op("act", I("activation", out=skT[:, half * 8:(half + 1) * 8, :].rearrange("p g i -> p (g i)"),
                          in_=psA[:, c0:c0 + 1024], func=AF.Copy),
                 reads=pg(c0, c0 + 1024), writes=["skT+"])
        RCH = 1024
        for (tab_ap, off) in ((pu_ap, 0), (pv_ap, D)):
            for r0 in range(0, 16384, RCH):
                S.dma("pool", I("dma_start", out=uv_ap[r0:r0 + RCH, off:off + D], in_=tab_ap[r0:r0 + RCH, :]),
                      "su_uv", writes=["uv+"], group=True)

        def rstd(ss_t, rs_t, nm, n_el):
            S.op("act", I("activation", out=rs_t[:], in_=ss_t[:], func=AF.Sqrt, scale=1.0 / n_el, bias=EPS),
                 reads=["ss" + nm], writes=["rs" + nm])
            S.op("dve", I("reciprocal", out=rs_t[:], in_=rs_t[:]), reads=["rs" + nm], writes=["rs" + nm])

        def transposes8(src, rsrc):
            for c in range(8):
                S.op("pe", I("transpose", out=psT[:, c * 128:(c + 1) * 128], in_=src[:, c * 128:(c + 1) * 128], identity=identb[:]),
                     reads=[rsrc, "identb"], writes=["BT" if c == 0 else "BT+"])
            S.op("act", I("activation", out=T8[:], in_=psT[:], func=AF.Copy), reads=["BT"], writes=["T8"])

        def top16(vals_ap, rvals, out_v, rv, out_i, ri, scratch, rscr, first):
            plus = "" if first else "+"
            S.op("dve", I("max", out=out_v[:, 0:8], in_=vals_ap), reads=[rvals], writes=[rv + plus])
            S.op("dve", I("match_replace", out=scratch, in_to_replace=out_v[:, 0:8], in_values=vals_ap, imm_value=-1e30),
                 reads=[rvals, rv], writes=[rscr])
            S.op("dve", I("max", out=out_v[:, 8:16], in_=scratch), reads=[rscr], writes=[rv + "+"])
            S.op("dve", I("max_index", out=out_i[:, 0:8], in_max=out_v[:, 0:8], in_values=vals_ap),
                 reads=[rvals, rv], writes=[ri + plus])
            S.op("dve", I("max_index", out=out_i[:, 8:16], in_max=out_v[:, 8:16], in_values=vals_ap),
                 reads=[rvals, rv], writes=[ri + "+"])

        def top16_pair(A, B):
            (va, rva, ova, rv, oia, ri, sca, rsa, firsta) = A
            (vb, rvb, ovb, _, oib, _, scb, rsb, _) = B
            pa_ = "" if firsta else "+"
            for (v, rvv, ov, oi, sc_, rs_, pl) in ((va, rva, ova, oia, sca, rsa, pa_), (vb, rvb, ovb, oib, scb, rsb, "+")):
                S.op("dve", I("max", out=ov[:, 0:8], in_=v), reads=[rvv], writes=[rv + pl])
            for (v, rvv, ov, oi, sc_, rs_, pl) in ((va, rva, ova, oia, sca, rsa, pa_), (vb, rvb, ovb, oib, scb, rsb, "+")):
                S.op("dve", I("match_replace", out=sc_, in_to_replace=ov[:, 0:8], in_values=v, imm_value=-1e30),
                     reads=[rvv, rv], writes=[rs_])
            for (v, rvv, ov, oi, sc_, rs_, pl) in ((va, rva, ova, oia, sca, rsa, pa_), (vb, rvb, ovb, oib, scb, rsb, "+")):
                S.op("dve", I("max", out=ov[:, 8:16], in_=sc_), reads=[rs_], writes=[rv + "+"])
            for (v, rvv, ov, oi, sc_, rs_, pl) in ((va, rva, ova, oia, sca, rsa, pa_), (vb, rvb, ovb, oib, scb, rsb, "+")):
                S.op("dve", I("max_index", out=oi[:, 0:8], in_max=ov[:, 0:8], in_values=v),
                     reads=[rvv, rv], writes=[ri + pl])
            for (v, rvv, ov, oi, sc_, rs_, pl) in ((va, rva, ova, oia, sca, rsa, pa_), (vb, rvb, ovb, oib, scb, rsb, "+")):
                S.op("dve", I("max_index", out=oi[:, 8:16], in_max=ov[:, 8:16], in_values=v),
                     reads=[rvv, rv], writes=[ri + "+"])

        def tr8(src, rsrc):
            for c in range(8):
                S.op("pe", I("transpose", out=psT[:, c * 128:(c + 1) * 128], in_=src[:, c * 128:(c + 1) * 128], identity=identb[:]),
                     reads=[rsrc, "identb"], writes=["BT" if c == 0 else "BT+"])
            yield 2
            S.op("act", I("activation", out=T8[:], in_=psT[:], func=AF.Copy), reads=["BT"], writes=["T8"])
            yield 2

        NCH = 18 * NT

        def prefetch():
            k = cnt["wp"]
            if k >= NCH:
                return
            cnt["wp"] += 1
            wb, rw = WBUF[k % NW], "wbuf%d" % (k % NW)
            S.dma("sp", I("dma_start", out=wb[:], in_=wch_ap[k % 18].rearrange("p (c k) -> p c k", c=8)),
                  "ldw%d" % (k % NW), reads=["wch"], writes=[rw])

        def stream_mm1(n, nbase):
            b = n - nbase
            for hh in range(2):
                k = cnt["w"]
                cnt["w"] += 1
                assert k % 18 == 2 * n + hh and k < cnt["wp"]
                wb, rw = WBUF[k % NW], "wbuf%d" % (k % NW)
                for c in range(8):
                    S.op("pe", I("matmul", out=psA[:, b * 512 + hh * 256:b * 512 + (hh + 1) * 256],
                                 lhsT=T8[:, c * 128:(c + 1) * 128], rhs=wb[:, c, :],
                                 start=(c == 0), stop=(c == 7)),
                         reads=["T8", rw], writes=bankp(b, not (c == 0 and hh == 0)))
                prefetch()

        def sqrt_(ss_t, rs_t, nm, n_el):
            S.op("act", I("activation", out=rs_t[:], in_=ss_t[:], func=AF.Ln, scale=1.0 / n_el, bias=EPS),
                 reads=["ss" + nm], writes=["rs" + nm])
            S.op("act", I("activation", out=rs_t[:], in_=rs_t[:], func=AF.Exp, scale=-0.5),
                 reads=["rs" + nm], writes=["rs" + nm])

        def recip_(rs_t, nm):
            pass

        def front(T):
            par = T & 1
            acc, racc = ACC[par], "acc%d" % par
            xn2, rxn2 = XN2[par], "xn2%d" % par
            slot = T % NSLOT
            S.dma("sp", I("dma_start", out=acc[:], in_=x_ap[T * 128:(T + 1) * 128, :]), "ldx%d" % par, writes=[racc])
            yield 1
            S.op("act", I("activation", out=xnb[:], in_=acc[:], func=AF.Square, accum_out=ss1[:]),
                 reads=[racc], writes=["xnb", "ss1"])
            sqrt_(ss1, rs1, "1", 1024)
            yield 1
            recip_(rs1, "1")
            S.op("dve", I("scalar_tensor_tensor", out=xnb[:], in0=acc[:], scalar=rs1[:, 0:1], in1=g1b[:],
                          op0=ALU.mult, op1=ALU.mult), reads=[racc, "rs1", "g1b"], writes=["xnb"])
            yield 1
            yield from tr8(xnb, "xnb")
            def evac_in(n):
                if n == 0:
                    S.op("act", I("activation", out=u_sb, in_=bank(0), func=AF.Gelu_apprx_tanh), reads=bankp(0), writes=["aA0"])
                elif n == 1:
                    S.op("act", I("activation", out=gv, in_=bank(1), func=AF.Gelu_apprx_tanh), reads=bankp(1), writes=["aB0"])
                elif n == 2:
                    S.op("act", I("activation", out=qf, in_=bank(2), func=AF.Copy), reads=bankp(2), writes=["aA1"])
                elif n == 3:
                    S.op("act", I("activation", out=kf, in_=bank(3), func=AF.Copy), reads=bankp(3), writes=["aA2"])
                else:
                    S.op("act", I("activation", out=Vr[:, slot * 8:(slot + 1) * 8, 0:64],
                                  in_=bank(4).rearrange("p (h d) -> p h d", h=8), func=AF.Copy),
                         reads=bankp(4), writes=["V%d+" % slot])

            for n in range(6):
                if n < 5:
                    stream_mm1(n, 0)
                    yield 3
                if n >= 1:
                    evac_in(n - 1)
                    yield 1
            S.op("dve", I("scalar_tensor_tensor", out=junkD[:, 0:512], in0=gv, scalar=1.0, in1=gv,
                          op0=ALU.mult, op1=ALU.mult, accum_out=ssv[:]), reads=["aB0"], writes=["junkD", "ssv"])
            h8 = "p (h d) -> p h d"
            S.op("dve", I("tensor_tensor", out=sqt, in0=qf, in1=qf, op=ALU.mult), reads=["aA1"], writes=["aA3"])
            S.op("dve", I("tensor_reduce", out=ssq[:], in_=sqt.rearrange(h8, h=8), axis=AX.X, op=ALU.add),
                 reads=["aA3"], writes=["ssq"])
            S.op("dve", I("tensor_tensor", out=sqt, in0=kf, in1=kf, op=ALU.mult), reads=["aA2"], writes=["aA3"])
            S.op("dve", I("tensor_reduce", out=ssk[:], in_=sqt.rearrange(h8, h=8), axis=AX.X, op=ALU.add),
                 reads=["aA3"], writes=["ssk"])
            yield 2
            sqrt_(ssv, rsv, "v", 512)
            sqrt_(ssq, rsq, "q", 64)
            sqrt_(ssk, rsk, "k", 64)
            yield 1
            recip_(rsv, "v")
            S.op("dve", I("scalar_tensor_tensor", out=vn[:], in0=gv, scalar=rsv[:, 0:1], in1=ngf[:],
                          op0=ALU.mult, op1=ALU.mult), reads=["aB0", "rsv", "ngf"], writes=["vn"])
            for (xf, rxf, rs_t, nm, gb, rgb, xo, rxo) in ((qf, "aA1", rsq, "q", qgb, "qgb", qn, "qn"),
                                                          (kf, "aA2", rsk, "k", kgb, "kgb", kn, "kn")):
                recip_(rs_t, nm)
                S.op("dve", I("tensor_tensor", out=sqt.rearrange(h8, h=8), in0=xf[:].rearrange(h8, h=8),
                              in1=rs_t[:].unsqueeze(2).to_broadcast([128, 8, 64]), op=ALU.mult),
                     reads=[rxf, "rs" + nm], writes=["aA3"])
                S.op("dve", I("tensor_tensor", out=xo[:], in0=sqt, in1=gb[:], op=ALU.mult),
                     reads=["aA3", rgb], writes=[rxo])
            yield 2
            for g in range(8):
                S.op("pe", I("matmul", out=psA[:, 4 * 512 + g * 64:4 * 512 + (g + 1) * 64], lhsT=wmT[:, g, :],
                             rhs=vn[:, g * 64:(g + 1) * 64], start=True, stop=True),
                     reads=["wmT", "vn"], writes=bankp(4, g != 0))
            for h in range(8):
                S.op("pe", I("transpose", out=psT[0:64, h * 128:(h + 1) * 128], in_=qn[:, h * 64:(h + 1) * 64], identity=identb[:]),
                     reads=["qn", "identb"], writes=["BT" if h == 0 else "BT+"])
            yield 1
            S.op("act", I("activation", out=qT[:].rearrange("p h t -> p (h t)"), in_=psT[0:64, :], func=AF.Copy),
                 reads=["BT"], writes=["qT"])
            S.op("dve", I("tensor_tensor", out=gv, in0=bank(4), in1=bsf[:], op=ALU.add),
                 reads=bankp(4) + ["bsf"], writes=["aB0"])
            S.op("dve", I("tensor_tensor", out=cat[:, 0:512], in0=gv, in1=u_sb, op=ALU.mult),
                 reads=["aB0", "aA0"], writes=["xnb"])
            yield 1
            for h in range(8):
                S.op("pe", I("transpose", out=psT[0:64, h * 128:(h + 1) * 128], in_=kn[:, h * 64:(h + 1) * 64], identity=identb[:]),
                     reads=["kn", "identb"], writes=["BT" if h == 0 else "BT+"])
            yield 1
            S.op("act", I("activation", out=kTr[:, slot * 8:(slot + 1) * 8, :].rearrange("p h t -> p (h t)"), in_=psT[0:64, :], func=AF.Copy),
                 reads=["BT"], writes=["kT%d" % slot])
            yield 1
            nr = min(5, T + 1)

            n4 = min(nr, 4)

            def s_cols(h, r):
                return (h % 2) * 512 + r * 128 if r < 4 else 1024 + (h % 2) * 128

            def scores(h):
                for r in range(nr):
                    sl = (T - r) % NSLOT
                    c0 = s_cols(h, r)
                    S.op("pe", I("matmul", out=psA[:, c0:c0 + 128],
                                 lhsT=kTr[:, sl * 8 + h, :], rhs=qT[:, h, :], start=True, stop=True),
                         reads=["kT%d" % sl, "qT"], writes=pg(c0, c0 + 128, r not in (0, 4)))

            scores(0)
            yield 1
            for h in range(8):
                c0 = s_cols(h, 0)
                S.op("act", I("activation", out=PTf[:, 0:n4 * 128], in_=psA[:, c0:c0 + n4 * 128], func=AF.Exp),
                     reads=pg(c0, c0 + n4 * 128), writes=["PTf"])
                if nr == 5:
                    c4 = s_cols(h, 4)
                    S.op("act", I("activation", out=PTf[:, 512:640], in_=psA[:, c4:c4 + 128], func=AF.Exp),
                         reads=pg(c4, c4 + 128), writes=["PTf+"])
                if h + 1 < 8:
                    scores(h + 1)
                yield 1
                pbuf, rpb = PTb[h % 2], "PTb%d" % (h % 2)
                S.op("dve", I("tensor_tensor", out=pbuf[:, 0:nr * 128], in0=PTf[:, 0:nr * 128], in1=EB[:, h, 0:nr * 128], op=ALU.mult),
                     reads=["PTf", "EB"], writes=[rpb])
                yield 1
                oc = (3 + h // 4) * 512 + (h % 4) * 128
                for r in range(nr):
                    sl = (T - r) % NSLOT
                    S.op("pe", I("matmul", out=psA[:, oc:oc + 65], lhsT=pbuf[:, r * 128:(r + 1) * 128],
                                 rhs=Vr[:, sl * 8 + h, :], start=(r == 0), stop=(r == nr - 1)),
                         reads=[rpb, "V%d" % sl], writes=pg(oc, oc + 128, not (r == 0 and h % 4 == 0)))
            yield 1
            for half in range(2):
                c0 = (3 + half) * 512
                bo = psA[:, c0:c0 + 512].rearrange("p (h c) -> p h c", h=4)
                S.op("dve", I("reciprocal", out=rden[:, half * 4:(half + 1) * 4].unsqueeze(2), in_=bo[:, :, 64:65]),
                     reads=pg(c0, c0 + 512), writes=["rden+"])
                S.op("dve", I("tensor_tensor", out=cat[:, 512 + half * 256:512 + (half + 1) * 256].rearrange("p (h d) -> p h d", h=4),
                              in0=bo[:, :, 0:64], in1=rden[:, half * 4:(half + 1) * 4].unsqueeze(2).to_broadcast([128, 4, 64]),
                              op=ALU.mult),
                     reads=pg(c0, c0 + 512) + ["rden"], writes=["xnb+"])
            yield 1
            yield from tr8(cat, "xnb")
            for n in range(2):
                for c in range(8):
                    S.op("pe", I("matmul", out=bank(n), lhsT=T8[:, c * 128:(c + 1) * 128], rhs=wout[:, c, n * 512:(n + 1) * 512],
                                 start=(c == 0), stop=(c == 7)),
                         reads=["T8", "wout"], writes=bankp(n, c != 0))
                yield 1
            S.op("dve", I("tensor_tensor", out=acc[:], in0=psA[:, 0:1024], in1=acc[:], op=ALU.add),
                 reads=pg(0, 1024) + [racc], writes=[racc])
            yield 1
            S.op("act", I("activation", out=xn2b[:], in_=acc[:], func=AF.Square, accum_out=ss2[:]),
                 reads=[racc], writes=["xnb", "ss2"])
            sqrt_(ss2, rs2, "2", 1024)
            yield 1
            recip_(rs2, "2")
            S.op("dve", I("scalar_tensor_tensor", out=xn2[:], in0=acc[:], scalar=rs2[:, 0:1], in1=g2b[:],
                          op0=ALU.mult, op1=ALU.mult), reads=[racc, "rs2", "g2b"], writes=[rxn2])
            yield 1
            S.op("act", I("activation", out=xn2b[:], in_=xn2[:], func=AF.Copy), reads=[rxn2], writes=["xnb"])
            yield 1
            yield from tr8(xn2b, "xnb")
            for n in range(5):
                if n < 4:
                    stream_mm1(5 + n, 5)
                    yield 3
                if n >= 1:
                    m = n - 1
                    S.op("act", I("activation", out=q2b[:, m * 512:(m + 1) * 512], in_=bank(m), func=AF.Copy),
                         reads=bankp(m), writes=["q2b" if m == 0 else "q2b+"])
                    yield 1
            for half in range(2):
                for i in range(8):
                    hp = half * 8 + i
                    S.op("pe", I("transpose", out=psT[:, i * 128:(i + 1) * 128], in_=q2b[:, hp * 128:(hp + 1) * 128], identity=identb[:]),
                         reads=["q2b", "identb"], writes=["BT" if i == 0 else "BT+"])
                yield 1
                S.op("act", I("activation", out=q2T[:, half * 1024:(half + 1) * 1024], in_=psT[:], func=AF.Copy),
                     reads=["BT"], writes=["q2T" if half == 0 else "q2T+"])
                yield 1
            sc = arenaB
            for n in range(4):
                for hp in range(4 * n, 4 * n + 4):
                    S.op("pe", I("matmul", out=psA[:, hp * 128:(hp + 1) * 128], lhsT=q2T[:, hp * 128:(hp + 1) * 128], rhs=skT[:, hp, :],
                                 start=True, stop=True),
                         reads=["q2T", "skT"], writes=pg(hp * 128, (hp + 1) * 128))
            yield 1
            for n in range(4):
                S.op("act", I("activation", out=sc[:, n * 512:(n + 1) * 512], in_=bank(n), func=AF.Copy),
                     reads=bankp(n), writes=["aB%d" % n])
                if n % 2 == 1:
                    yield 1
            for hp in range(0, 16, 2):
                top16_pair((sc[:, hp * 128:(hp + 1) * 128], "aB%d" % (hp // 4), s_top[:, hp * 16:(hp + 1) * 16], "s_top",
                            i_top[:, hp * 16:(hp + 1) * 16], "i_top", scr[:], "scr", hp == 0),
                           (sc[:, (hp + 1) * 128:(hp + 2) * 128], "aB%d" % ((hp + 1) // 4), s_top[:, (hp + 1) * 16:(hp + 2) * 16], "s_top",
                            i_top[:, (hp + 1) * 16:(hp + 2) * 16], "i_top", scrb[:], "scrb", False))
                yield 3
            S.op("dve", I("tensor_copy", out=itf[:], in_=i_top[:]), reads=["i_top"], writes=["itf"])
            st4 = s_top[:].rearrange("p (h q k) -> p h q k", h=8, q=2)
            cand = arenaA
            S.op("dve", I("tensor_tensor", out=cand[:].rearrange("p (h a b) -> p h a b", h=8, a=16),
                          in0=st4[:, :, 0, :].unsqueeze(3).to_broadcast([128, 8, 16, 16]),
                          in1=st4[:, :, 1, :].unsqueeze(2).to_broadcast([128, 8, 16, 16]), op=ALU.add),
                 reads=["s_top"], writes=AA)
            yield 2
            for h in range(0, 8, 2):
                top16_pair((cand[:, h * 256:(h + 1) * 256], "aA%d" % (h // 2), best[:, h * 16:(h + 1) * 16], "best",
                            pos[:, h * 16:(h + 1) * 16], "pos", scr2[:], "scr2", h == 0),
                           (cand[:, (h + 1) * 256:(h + 2) * 256], "aA%d" % ((h + 1) // 2), best[:, (h + 1) * 16:(h + 2) * 16], "best",
                            pos[:, (h + 1) * 16:(h + 2) * 16], "pos", scr2b[:], "scr2b", False))
                yield 4
            b3 = best[:].rearrange("p (h k) -> p h k", h=8)
            S.op("dve", I("tensor_tensor", out=bm[:].rearrange("p (h k) -> p h k", h=8), in0=b3,
                          in1=b3[:, :, 0:1].to_broadcast([128, 8, 16]), op=ALU.subtract), reads=["best"], writes=["bm"])
            S.op("act", I("activation", out=ge[:], in_=bm[:], func=AF.Exp), reads=["bm"], writes=["ge"])
            S.op("dve", I("tensor_single_scalar", out=pa[:], in_=pos[:].bitcast(I32), scalar=4, op=ALU.arith_shift_right),
                 reads=["pos"], writes=["pa"])
            S.op("dve", I("tensor_single_scalar", out=pb[:], in_=pos[:].bitcast(I32), scalar=15, op=ALU.bitwise_and),
                 reads=["pos"], writes=["pb"])
            S.op("dve", I("tensor_copy", out=paf[:], in_=pa[:]), reads=["pa"], writes=["paf"])
            S.op("dve", I("tensor_copy", out=pbf[:], in_=pb[:]), reads=["pb"], writes=["pbf"])
            yield 1
            E = arenaB
            E3 = E[:].rearrange("p (n a) -> p n a", a=16)
            E4 = E[:].rearrange("p (h k a) -> p h k a", h=8, k=16)
            it4 = itf[:].rearrange("p (h q a) -> p h q a", h=8, q=2)
            for (pf, rpf, q, eo, reo) in ((paf, "paf", 0, e0, "e0"), (pbf, "pbf", 1, e1, "e1")):
                S.op("dve", I("tensor_tensor", out=E3, in0=pf[:].unsqueeze(2).to_broadcast([128, 128, 16]),
                              in1=iotaf[:, 0:16].unsqueeze(1).to_broadcast([128, 128, 16]), op=ALU.is_equal),
                     reads=[rpf, "iotaf"], writes=AB)
                yield 2
                S.op("dve", I("tensor_tensor", out=E4, in0=E4, in1=it4[:, :, q, :].unsqueeze(2).to_broadcast([128, 8, 16, 16]),
                              op=ALU.mult), reads=AB + ["itf"], writes=AB)
                yield 2
                S.op("dve", I("tensor_reduce", out=eo[:], in_=E3, axis=AX.X, op=ALU.add), reads=AB, writes=[reo])
                yield 2
            S.op("dve", I("scalar_tensor_tensor", out=EIDX[par][:], in0=e0[:], scalar=128.0, in1=e1[:],
                          op0=ALU.mult, op1=ALU.add), reads=["e0", "e1"], writes=["eidx%d" % par])
            S.op("dve", I("tensor_reduce", out=gs[:], in_=ge[:].rearrange("p (h k) -> p h k", h=8), axis=AX.X, op=ALU.add),
                 reads=["ge"], writes=["gs"])
            S.op("dve", I("tensor_scalar", out=gs[:], in0=gs[:], scalar1=2.0, scalar2=None, op0=ALU.mult), reads=["gs"], writes=["gs"])
            S.op("dve", I("reciprocal", out=gs[:], in_=gs[:]), reads=["gs"], writes=["gs"])
            S.op("dve", I("tensor_tensor", out=GN[par][:].rearrange("p (h k) -> p h k", h=8),
                          in0=ge[:].rearrange("p (h k) -> p h k", h=8),
                          in1=gs[:].unsqueeze(2).to_broadcast([128, 8, 16]), op=ALU.mult),
                 reads=["ge", "gs"], writes=["gn%d" % par])
            yield 1

        def gather(eidx, reidx, j):
            k = cnt["g"]
            cnt["g"] += 1
            s = k % NG
            sem = "g%de%d" % (s, (k // NG) // 1000)
            S.dma("pool", I("indirect_dma_start", out=gbuf[:, s, :], out_offset=None, in_=uv_ap,
                            in_offset=bass.IndirectOffsetOnAxis(ap=eidx[:, j:j + 1], axis=0)),
                  sem, reads=[reidx, "uv"], writes=["gbuf%d" % s])
            return s

        GS = 4
        GK = 0.044715
        GC = 0.7978845608028654
        LAG_B1, LAG_B2, LAG_C = 1, 2, 3

        pend = {"add": None, "store": None}

        def finish_add():
            T = pend["add"]
            if T is None:
                return
            pend["add"] = None
            par = T & 1
            acc, racc = ACC[par], "acc%d" % par
            S.op("dve", I("tensor_tensor", out=acc[:], in0=psA[:, ACCC:ACCC + 1024], in1=acc[:], op=ALU.add),
                 reads=pg(ACCC, ACCC + 1024) + [racc], writes=[racc])
            pend["store"] = T

        def finish_store():
            T = pend["store"]
            if T is None:
                return
            pend["store"] = None
            par = T & 1
            S.dma("pool", I("dma_start", out=y_ap[T * 128:(T + 1) * 128, :], in_=ACC[par][:]), "st%d" % par,
                  reads=["acc%d" % par])

        def experts(T):
            par = T & 1
            acc, racc = ACC[par], "acc%d" % par
            xn2, rxn2 = XN2[par], "xn2%d" % par
            eidx, reidx = EIDX[par], "eidx%d" % par
            hpre, rh = HPRE[par], "hpre%d" % par
            wgt, rw = WGT[par], "wgt%d" % par
            rgn = "gn%d" % par
            slot_of = {}
            NGRP = 128 // GS
            assert GS - 1 + LAG_C > 4
            for j in range(128 + GS + LAG_C):
                if j == 4:
                    finish_add()
                if j == 10:
                    finish_store()
                if j < 128:
                    g = j // GS
                    rhg = "%sg%d" % (rh, g)
                    s_ = gather(eidx, reidx, j)
                    slot_of[j] = s_
                    S.op("dve", I("scalar_tensor_tensor", out=junkD[:], in0=gbuf[:, s_, 0:D], scalar=1.0, in1=xn2[:],
                                  op0=ALU.mult, op1=ALU.mult, accum_out=hpre[:, j:j + 1]),
                         reads=["gbuf%d" % s_, rxn2], writes=["junkD", rhg + "+"])
                ja = j - (GS - 1)
                if ja >= 0 and ja % GS == 0 and ja // GS < NGRP:
                    g = ja // GS
                    j0, j1 = g * GS, (g + 1) * GS
                    rhg = "%sg%d" % (rh, g)
                    hs = hpre[:, j0:j1]
                    S.op("act", I("activation", out=tq[:, j0:j1], in_=hs, func=AF.Square, scale=GK ** 0.5),
                         reads=[rhg], writes=["tq%d" % g])
                    S.op("dve", I("tensor_tensor", out=xg[:, j0:j1], in0=hs, in1=GN[par][:, j0:j1], op=ALU.mult),
                         reads=[rhg, rgn], writes=["xg%d" % g])
                jb = j - (GS - 1) - LAG_B1
                if jb >= 0 and jb % GS == 0 and jb // GS < NGRP:
                    g = jb // GS
                    j0, j1 = g * GS, (g + 1) * GS
                    rhg = "%sg%d" % (rh, g)
                    S.op("dve", I("scalar_tensor_tensor", out=tq[:, j0:j1], in0=tq[:, j0:j1], scalar=1.0, in1=hpre[:, j0:j1],
                                  op0=ALU.add, op1=ALU.mult), reads=["tq%d" % g, rhg], writes=["tq%d" % g])
                    S.op("act", I("activation", out=hg[:, j0:j1], in_=tq[:, j0:j1], func=AF.Tanh, scale=GC),
                         reads=["tq%d" % g], writes=["hg%d" % g])
                jb = j - (GS - 1) - LAG_B2
                if jb >= 0 and jb % GS == 0 and jb // GS < NGRP:
                    g = jb // GS
                    j0, j1 = g * GS, (g + 1) * GS
                    S.op("dve", I("scalar_tensor_tensor", out=wgt[:, j0:j1], in0=hg[:, j0:j1], scalar=1.0, in1=xg[:, j0:j1],
                                  op0=ALU.add, op1=ALU.mult), reads=["hg%d" % g, "xg%d" % g], writes=["%sg%d" % (rw, g)])
                q = j - (GS - 1) - LAG_C
                jcs = []
                if q >= 0 and q % GS < 2 and q // GS < NGRP:
                    jcs = [(q // GS) * GS + 2 * (q % GS), (q // GS) * GS + 2 * (q % GS) + 1]
                for jc in jcs:
                    s_ = slot_of.pop(jc)
                    rwg = "%sg%d" % (rw, jc // GS)
                    k = cnt["d"] % ND
                    cnt["d"] += 1
                    S.op("act", I("activation", out=DIAG[k][:], in_=identb[:], func=AF.Copy, scale=wgt[:, jc:jc + 1]),
                         reads=["identb", rwg], writes=["diag%d" % k])
                    for half in range(2):
                        c0 = ACCC + half * 512
                        S.op("pe", I("matmul", out=psA[:, c0:c0 + 512], lhsT=DIAG[k][:],
                                     rhs=gbuf[:, s_, D + half * 512:D + (half + 1) * 512],
                                     start=(jc == 0), stop=(jc == 127)),
                             reads=["diag%d" % k, "gbuf%d" % s_], writes=pg(c0, c0 + 512, jc != 0))
                yield
            assert not slot_of
            pend["add"] = T
            yield

        def drain(g):
            for _ in g:
                pass

        for _ in range(NW):
            prefetch()
        if not interleave:
            for T in range(NT):
                drain(front(T))
                drain(experts(T))
        else:
            units = sum(front(0))
            HEAD = 13
            total = 128 + GS + LAG_C + 1
            ratio = (total - HEAD) / float(units) * RATIO
            for T in range(NT):
                ge_ = experts(T)
                e_alive = True
                for _ in range(HEAD):
                    next(ge_)
                if T + 1 < NT:
                    credit = 0.0
                    for n in front(T + 1):
                        credit += n * ratio
                        while credit >= 1.0 and e_alive:
                            credit -= 1.0
                            try:
                                next(ge_)
                            except StopIteration:
                                e_alive = False
                if e_alive:
                    drain(ge_)

        finish_add()
        finish_store()
        toks = [(("dma", "st%d" % p), S.dma_count["st%d" % p]) for p in range(2) if ("st%d" % p) in S.dma_count]
        S.final_wait("pool", toks)
        S.emit()
    return nc


_PARAM_SHAPES = {
    "ln_mix_g": (1, D), "w_in": (D, 2560), "gmlp_norm_g": (1, 512), "gmlp_w_s": (1024, 128),
    "gmlp_b_s": (8, 128), "q_norm_g": (1, 64), "k_norm_g": (1, 64), "rel_bias": (8, 257),
    "w_out": (D, D), "ln_ffn_g": (1, D), "peer_w_query": (D, 2048), "peer_sub_keys": (2048, 128),
    "peer_u": (16384, D), "peer_v": (16384, D),
}


def _params(inputs):
    return {k: np.ascontiguousarray(np.asarray(inputs[k], dtype=np.float32).reshape(shp))
            for k, shp in _PARAM_SHAPES.items()}


def kernel(**inputs):
    x = np.ascontiguousarray(np.asarray(inputs["x"], dtype=np.float32))
    B = x.shape[0]
    common = _params(inputs)
    nc = build_nc(32)
    in_maps = [dict(common, x=x[b]) for b in range(B)]
    res = run_bass_kernel_spmd(nc, in_maps, core_ids=list(range(B)))
    return np.stack([np.asarray(r["y"]) for r in res.results], axis=0).astype(np.float32)
```
